# Optimizing a Trainium2 kernel written in Bass

```python
import numpy as np
import jax
import jax.numpy as jnp
from jax import lax

D_MODEL = 2048
BATCH = 4
SEQ = 8192
DEPTH = 1

MEM_LEN = 256
EPS = 1e-6
NEG = -1e30
BIG = 1e30
NSA_HEADS = 8
NSA_KV_HEADS = 2
NSA_GROUP = NSA_HEADS // NSA_KV_HEADS
NSA_DK = 128
NSA_DV = 128
CMP_LEN = 32
CMP_STRIDE = 16
CMP_HIDDEN = 256
SEL_BLOCK = 64
SEL_TOP = 16
WINDOW = 512
NSA_QBLOCK = 64
GLA_HEADS = 4
GLA_DK = 128
GLA_DV = 256
GLA_GATE_RANK = 16
GLA_GATE_NORM = 16.0
GLA_CHUNK = 64
D_MIX = NSA_HEADS * NSA_DV + GLA_HEADS * GLA_DV
MEM_HEADS = 4
MEM_DH = 128
D_FF = -(-8 * D_MODEL // (3 * 256)) * 256
IN_SIZES = (
    NSA_HEADS * NSA_DK,
    NSA_KV_HEADS * NSA_DK, NSA_KV_HEADS * NSA_DV,
    NSA_KV_HEADS * NSA_DK, NSA_KV_HEADS * NSA_DV,
    NSA_KV_HEADS * NSA_DK, NSA_KV_HEADS * NSA_DV,
    3 * NSA_HEADS,
    GLA_HEADS * GLA_DK, GLA_HEADS * GLA_DK,
    GLA_HEADS * GLA_DV,
    GLA_GATE_RANK,
    GLA_HEADS * GLA_DV,
)
D_IN = sum(IN_SIZES)

kernel_name = 'hymba_nsa_gla_alibi_memory_block'


def rms_norm(u, g):
    uf = u.astype(jnp.float32)
    y = uf * lax.rsqrt(jnp.mean(uf * uf, axis=-1, keepdims=True) + EPS)
    return (y * g.astype(jnp.float32)).astype(u.dtype)


def alibi_slopes(n):
    return jnp.exp2(-8.0 * jnp.arange(1, n + 1, dtype=jnp.float32) / n)


def split_points():
    return np.cumsum(np.array(IN_SIZES))[:-1].tolist()


def nsa_mixer(q, k_cmp, v_cmp, k_slc, v_slc, k_win, v_win, gates,
              g_q, g_kc, g_ks, g_kw, pe_k, pe_v, w_ck1, w_ck2, w_cv1, w_cv2):
    f32 = jnp.float32
    B, T = q.shape[0], q.shape[1]
    HKV, G, QB = NSA_KV_HEADS, NSA_GROUP, NSA_QBLOCK
    n_cmp = (T - CMP_LEN) // CMP_STRIDE + 1
    n_sel = T // SEL_BLOCK
    n_top = min(SEL_TOP, n_sel)
    n_qb = T // QB
    ratio = SEL_BLOCK // CMP_STRIDE
    lead = CMP_LEN // CMP_STRIDE - 1

    tok = jnp.arange(n_cmp)[:, None] * CMP_STRIDE + jnp.arange(CMP_LEN)[None, :]

    def compress(u, pe, w1, w2):
        blk = u[:, tok] + pe[None, None, :, None, :]
        blk = blk.transpose(0, 1, 3, 2, 4).reshape(B, n_cmp, HKV, -1)
        return (jax.nn.silu(blk @ w1) @ w2).transpose(0, 2, 1, 3)

    kc = rms_norm(compress(k_cmp, pe_k, w_ck1, w_ck2), g_kc)
    vc = compress(v_cmp, pe_v, w_cv1, w_cv2)
    cmp_end = jnp.arange(n_cmp) * CMP_STRIDE + (CMP_LEN - 1)
    cmp_mid = jnp.arange(n_cmp).astype(f32) * CMP_STRIDE + 0.5 * (CMP_LEN - 1)

    r = np.arange(ratio + lead)
    st = CMP_STRIDE * (r - lead)
    ov = jnp.asarray((np.minimum(st + CMP_LEN, SEL_BLOCK) - np.maximum(st, 0)) / CMP_STRIDE, f32)
    sel_idx = jnp.arange(n_sel)[:, None] * ratio + jnp.asarray(r)[None, :]

    ksb = rms_norm(k_slc, g_ks).reshape(B, n_sel, SEL_BLOCK, HKV, NSA_DK).transpose(0, 3, 1, 2, 4)
    vsb = v_slc.reshape(B, n_sel, SEL_BLOCK, HKV, NSA_DV).transpose(0, 3, 1, 2, 4)
    pad_t = ((0, 0), (0, 0), (WINDOW, 0), (0, 0))
    kwp = jnp.pad(rms_norm(k_win, g_kw).transpose(0, 2, 1, 3), pad_t)
    vwp = jnp.pad(v_win.transpose(0, 2, 1, 3), pad_t)

    slopes = alibi_slopes(NSA_HEADS).reshape(HKV, G)
    sl5 = slopes[None, :, :, None, None]
    sl6 = slopes[None, :, :, None, None, None]
    jb = jnp.arange(n_sel)
    bi = jnp.arange(B)[:, None, None, None]
    hi = jnp.arange(HKV)[None, :, None, None]

    qn = rms_norm(q, g_q) * (NSA_DK ** -0.5)
    qs = qn.reshape(B, n_qb, QB, HKV, G, NSA_DK).transpose(1, 0, 3, 4, 2, 5)
    gs = gates.reshape(B, n_qb, QB, HKV, G, 3).transpose(1, 0, 3, 4, 2, 5)

    def block(args):
        c, qb, gb = args
        t = c * QB + jnp.arange(QB)
        s = jnp.einsum('bkgtd,bknd->bkgtn', qb, kc).astype(f32)
        vis = t[:, None] >= cmp_end[None, :]
        s = jnp.where(vis, s - sl5 * (t[:, None].astype(f32) - cmp_mid[None, :]), NEG)
        p = jax.nn.softmax(s, axis=-1)
        p = jnp.where(jnp.any(vis, axis=-1)[:, None], p, 0.0)
        o_cmp = jnp.einsum('bkgtn,bknd->bkgtd', p.astype(vc.dtype), vc)
        imp = jnp.pad(p.sum(2), ((0, 0), (0, 0), (0, 0), (lead, lead)))
        imp = jnp.einsum('bktjr,r->bktj', imp[..., sel_idx], ov)
        cur = (t // SEL_BLOCK)[:, None]
        forced = (jb == 0) | (jb == cur) | (jb == cur - 1)
        score = jnp.where(forced, BIG, jnp.where(jb <= cur, imp, NEG))
        top_s, top_i = lax.top_k(score, n_top)
        ks_g = ksb[bi, hi, top_i]
        vs_g = vsb[bi, hi, top_i]
        pos = top_i[..., None] * SEL_BLOCK + jnp.arange(SEL_BLOCK)
        tt = t[:, None, None]
        ok = (top_s > 0.5 * NEG)[..., None] & (pos <= tt)
        s = jnp.einsum('bkgtd,bktnsd->bkgtns', qb, ks_g).astype(f32)
        s = jnp.where(ok[:, :, None], s - sl6 * (tt - pos).astype(f32)[:, :, None], NEG)
        p = jax.nn.softmax(s.reshape(s.shape[:4] + (-1,)), axis=-1).reshape(s.shape)
        o_slc = jnp.einsum('bkgtns,bktnsd->bkgtd', p.astype(vs_g.dtype), vs_g)
        kw = lax.dynamic_slice_in_dim(kwp, c * QB, QB + WINDOW, axis=2)
        vw = lax.dynamic_slice_in_dim(vwp, c * QB, QB + WINDOW, axis=2)
        pos_w = c * QB - WINDOW + jnp.arange(QB + WINDOW)
        dw = t[:, None] - pos_w[None, :]
        okw = (dw >= 0) & (dw < WINDOW) & (pos_w >= 0)[None, :]
        s = jnp.einsum('bkgtd,bksd->bkgts', qb, kw).astype(f32)
        s = jnp.where(okw, s - sl5 * dw.astype(f32), NEG)
        p = jax.nn.softmax(s, axis=-1)
        o_win = jnp.einsum('bkgts,bksd->bkgtd', p.astype(vw.dtype), vw)
        return gb[..., 0:1] * o_cmp + gb[..., 1:2] * o_slc + gb[..., 2:3] * o_win

    out = lax.map(block, (jnp.arange(n_qb), qs, gs))
    return out.transpose(1, 0, 4, 2, 3, 5).reshape(B, T, NSA_HEADS, NSA_DV)


def gla_mixer(q, k, v, log_a):
    B, T, H, DK = q.shape
    DV = v.shape[-1]
    C = GLA_CHUNK
    n = T // C

    def chunks(u):
        return u.astype(jnp.float32).reshape(B, n, C, H, -1).transpose(1, 0, 3, 2, 4)

    causal = jnp.tril(jnp.ones((C, C), bool))[:, :, None]

    def step(S, inp):
        qc, kc, vc, ac = inp
        b = jnp.cumsum(ac, axis=2)
        decay = jnp.exp(jnp.where(causal, b[:, :, :, None, :] - b[:, :, None, :, :], -jnp.inf))
        attn = jnp.einsum('bhid,bhjd,bhijd->bhij', qc, kc, decay)
        o = (jnp.einsum('bhij,bhjv->bhiv', attn, vc)
             + jnp.einsum('bhid,bhdv->bhiv', qc * jnp.exp(b), S))
        b_end = b[:, :, -1:, :]
        S = (S * jnp.exp(b_end[:, :, 0, :, None])
             + jnp.einsum('bhjd,bhjv->bhdv', kc * jnp.exp(b_end - b), vc))
        return S, o

    S0 = jnp.zeros((B, H, DK, DV), jnp.float32)
    _, o = lax.scan(step, S0, (chunks(q), chunks(k), chunks(v), chunks(log_a)))
    return o.transpose(1, 0, 3, 2, 4).reshape(B, T, H, DV)


def hybrid_layer(x, mem, g_mix, w_in, b_nsa_gate, g_q, g_kc, g_ks, g_kw, pe_k, pe_v,
                 w_ck1, w_ck2, w_cv1, w_cv2, g_nsa_out, w_gk2, b_gk, g_gla_out, w_out,
                 g_cross, g_mem, w_cq, w_ck, w_cv, g_cq, g_ck, w_co, g_ffn, w_gu, w_down):
    B, T, _ = x.shape
    M = mem.shape[1]
    h = rms_norm(x, g_mix)
    (q, k_c, v_c, k_s, v_s, k_w, v_w, g_logit,
     q_l, k_l, v_l, a_l, r_l) = jnp.split(h @ w_in, split_points(), axis=-1)

    def heads(u, nh):
        return u.reshape(B, T, nh, -1)

    hkv = NSA_KV_HEADS
    gates = jax.nn.sigmoid(g_logit + b_nsa_gate).reshape(B, T, NSA_HEADS, 3)
    o_nsa = nsa_mixer(heads(q, NSA_HEADS), heads(k_c, hkv), heads(v_c, hkv), heads(k_s, hkv),
                      heads(v_s, hkv), heads(k_w, hkv), heads(v_w, hkv), gates,
                      g_q, g_kc, g_ks, g_kw, pe_k, pe_v, w_ck1, w_ck2, w_cv1, w_cv2)
    o_nsa = rms_norm(o_nsa, g_nsa_out).reshape(B, T, -1)

    log_a = jax.nn.log_sigmoid((a_l @ w_gk2 + b_gk).astype(jnp.float32)) / GLA_GATE_NORM
    o_gla = gla_mixer(heads(q_l, GLA_HEADS) * (GLA_DK ** -0.5), heads(k_l, GLA_HEADS),
                      heads(v_l, GLA_HEADS), heads(log_a, GLA_HEADS))
    o_gla = rms_norm(o_gla, g_gla_out).astype(x.dtype).reshape(B, T, -1) * jax.nn.silu(r_l)
    x = x + jnp.concatenate([o_nsa, o_gla], axis=-1) @ w_out

    hq = rms_norm(x, g_cross)
    hm = rms_norm(mem, g_mem)
    cq = rms_norm((hq @ w_cq).reshape(B, T, MEM_HEADS, MEM_DH), g_cq) * (MEM_DH ** -0.5)
    ck = rms_norm((hm @ w_ck).reshape(B, M, MEM_HEADS, MEM_DH), g_ck)
    cv = (hm @ w_cv).reshape(B, M, MEM_HEADS, MEM_DH)
    s = jnp.einsum('bthd,bmhd->bhtm', cq, ck).astype(jnp.float32)
    p = jax.nn.softmax(s, axis=-1).astype(cv.dtype)
    x = x + jnp.einsum('bhtm,bmhd->bthd', p, cv).reshape(B, T, -1) @ w_co

    hf = rms_norm(x, g_ffn)
    gg, uu = jnp.split(hf @ w_gu, 2, axis=-1)
    return x + (jax.nn.silu(gg) * uu) @ w_down


def setup_inputs(seed: int = 0) -> dict:
    key = jax.random.key(seed)
    keys = iter(jax.random.split(key, 40))
    L = DEPTH

    def nrm(shape, scale):
        return scale * jax.random.normal(next(keys), shape, jnp.float32)

    def gain(n):
        return 1.0 + nrm((L, n), 0.05)

    return {
        'x': nrm((BATCH, SEQ, D_MODEL), 1.0),
        'mem': nrm((BATCH, MEM_LEN, D_MODEL), 1.0),
        'g_mix': gain(D_MODEL),
        'w_in': nrm((L, D_MODEL, D_IN), D_MODEL ** -0.5),
        'b_nsa_gate': nrm((L, 3 * NSA_HEADS), 0.1),
        'g_q': gain(NSA_DK),
        'g_kc': gain(NSA_DK),
        'g_ks': gain(NSA_DK),
        'g_kw': gain(NSA_DK),
        'pe_k': nrm((L, CMP_LEN, NSA_DK), 0.5),
        'pe_v': nrm((L, CMP_LEN, NSA_DV), 0.5),
        'w_ck1': nrm((L, CMP_LEN * NSA_DK, CMP_HIDDEN), (CMP_LEN * NSA_DK) ** -0.5),
        'w_ck2': nrm((L, CMP_HIDDEN, NSA_DK), CMP_HIDDEN ** -0.5),
        'w_cv1': nrm((L, CMP_LEN * NSA_DV, CMP_HIDDEN), (CMP_LEN * NSA_DV) ** -0.5),
        'w_cv2': nrm((L, CMP_HIDDEN, NSA_DV), CMP_HIDDEN ** -0.5),
        'g_nsa_out': gain(NSA_DV),
        'w_gk2': nrm((L, GLA_GATE_RANK, GLA_HEADS * GLA_DK), GLA_GATE_RANK ** -0.5),
        'b_gk': nrm((L, GLA_HEADS * GLA_DK), 0.1),
        'g_gla_out': gain(GLA_DV),
        'w_out': nrm((L, D_MIX, D_MODEL), D_MIX ** -0.5),
        'g_cross': gain(D_MODEL),
        'g_mem': gain(D_MODEL),
        'w_cq': nrm((L, D_MODEL, MEM_HEADS * MEM_DH), D_MODEL ** -0.5),
        'w_ck': nrm((L, D_MODEL, MEM_HEADS * MEM_DH), D_MODEL ** -0.5),
        'w_cv': nrm((L, D_MODEL, MEM_HEADS * MEM_DH), D_MODEL ** -0.5),
        'g_cq': gain(MEM_DH),
        'g_ck': gain(MEM_DH),
        'w_co': nrm((L, MEM_HEADS * MEM_DH, D_MODEL), (MEM_HEADS * MEM_DH) ** -0.5),
        'g_ffn': gain(D_MODEL),
        'w_gu': nrm((L, D_MODEL, 2 * D_FF), D_MODEL ** -0.5),
        'w_down': nrm((L, D_FF, D_MODEL), D_FF ** -0.5),
    }


def reference(x, mem, g_mix, w_in, b_nsa_gate, g_q, g_kc, g_ks, g_kw, pe_k, pe_v,
              w_ck1, w_ck2, w_cv1, w_cv2, g_nsa_out, w_gk2, b_gk, g_gla_out, w_out,
              g_cross, g_mem, w_cq, w_ck, w_cv, g_cq, g_ck, w_co, g_ffn, w_gu, w_down):
    for l in range(DEPTH):
        x = hybrid_layer(x, mem, g_mix[l], w_in[l], b_nsa_gate[l], g_q[l], g_kc[l], g_ks[l],
                         g_kw[l], pe_k[l], pe_v[l], w_ck1[l], w_ck2[l], w_cv1[l], w_cv2[l],
                         g_nsa_out[l], w_gk2[l], b_gk[l], g_gla_out[l], w_out[l],
                         g_cross[l], g_mem[l], w_cq[l], w_ck[l], w_cv[l], g_cq[l], g_ck[l],
                         w_co[l], g_ffn[l], w_gu[l], w_down[l])
    return x
```

```python
import os
import numpy as np
import ml_dtypes
from contextlib import ExitStack
import concourse.bass as bass
import concourse.mybir as mybir
from concourse.bass_utils import run_bass_kernel_spmd

F32 = mybir.dt.float32
BF16 = mybir.dt.bfloat16
ALU = mybir.AluOpType
AF = mybir.ActivationFunctionType
AX = mybir.AxisListType

T = 8192
TO = 4096
D = 2048
NT = 64
NTO = 32
EPS = 1e-6
DFF = 5632
NEGM = -30000.0

COMPUTE = ("pe", "act", "dve", "pool")
QUEUES = ("sp", "pool", "act")
DMA_RING = 8


class Res:
    __slots__ = ("w", "r")

    def __init__(self):
        self.w = None
        self.r = []


class Op:
    __slots__ = ("eng", "fn", "deps", "dma", "sig", "sigval", "idx", "dsem", "dval", "done")

    def __init__(self, eng, fn, dma):
        self.eng = eng
        self.fn = fn
        self.dma = dma
        self.deps = []
        self.sig = False
        self.sigval = 0
        self.done = False


class Prog:
    def __init__(self, nc, sems, dsems):
        self.nc = nc
        self.sems = sems
        self.dsems = dsems
        self.pending = []
        self.dma_ops = {q: [] for q in QUEUES}
        self.cnt = {e: 0 for e in COMPUTE}
        self.waited = {e: {} for e in ("pe", "act", "dve", "pool", "sp")}

    def _add(self, eng, fn, reads, writes, dma):
        op = Op(eng, fn, dma)
        deps = set()
        for r in reads:
            if r.w is not None:
                deps.add(r.w)
        for w in writes:
            if w.w is not None:
                deps.add(w.w)
            for rd in w.r:
                deps.add(rd)
        op.deps = list(deps)
        for r in reads:
            r.r.append(op)
        for w in writes:
            w.w = op
            w.r = []
        if dma:
            q = self.dma_ops[eng]
            op.idx = len(q)
            if op.idx >= DMA_RING:
                op.deps.append(q[op.idx - DMA_RING])
            q.append(op)
        self.pending.append(op)
        return op

    def op(self, eng, fn, reads=(), writes=()):
        return self._add(eng, fn, reads, writes, False)

    def dma(self, eng, out, in_, reads=(), writes=(), **kw):
        return self._add(eng, lambda e: e.dma_start(out=out, in_=in_, **kw), reads, writes, True)

    def flush(self, final=False):
        nc = self.nc
        ops = self.pending
        self.pending = []
        for op in ops:
            for d in op.deps:
                if d.dma or d.done:
                    continue
                if d.eng == op.eng and d.eng == "pe" and not op.dma:
                    continue
                d.sig = True
        for op in ops:
            if op.dma:
                op.dsem = self.dsems[op.eng][op.idx % DMA_RING]
                op.dval = 16 * (op.idx // DMA_RING + 1)
            elif op.sig:
                self.cnt[op.eng] += 1
                op.sigval = self.cnt[op.eng]
        by_eng = {e: [] for e in ("pe", "act", "dve", "pool", "sp")}
        for op in ops:
            by_eng[op.eng].append(op)

        def run_engine(ename, eng):
            waited = self.waited[ename]
            for op in by_eng[ename]:
                need = {}
                for d in op.deps:
                    if d.dma:
                        key = ("d", d.eng, d.idx % DMA_RING)
                        sem, val = d.dsem, d.dval
                    else:
                        if d.done and not d.sig:
                            continue
                        if d.eng == ename and ename == "pe" and not op.dma:
                            continue
                        key = ("c", d.eng)
                        sem, val = self.sems[d.eng], d.sigval
                    if waited.get(key, 0) >= val:
                        continue
                    if key not in need or need[key][1] < val:
                        need[key] = (sem, val)
                for key, (sem, val) in need.items():
                    eng.wait_ge(sem, val)
                    waited[key] = val
                ins = op.fn(eng)
                if op.dma:
                    ins.then_inc(op.dsem, 16)
                elif op.sig:
                    ins.then_inc(self.sems[ename], 1)
            if ename in self.dma_ops:
                last = {}
                for d in self.dma_ops[ename]:
                    last[d.idx % DMA_RING] = d
                for k, d in last.items():
                    if waited.get(("d", ename, k), 0) < d.dval:
                        eng.wait_ge(d.dsem, d.dval)
                        waited[("d", ename, k)] = d.dval

        with nc.Block() as block:
            @block.tensor
            def _(e):
                run_engine("pe", e)

            @block.scalar
            def _(e):
                run_engine("act", e)

            @block.vector
            def _(e):
                run_engine("dve", e)

            @block.gpsimd
            def _(e):
                run_engine("pool", e)

            @block.sync
            def _(e):
                run_engine("sp", e)
        for op in ops:
            op.done = True


_UID = [0]


def un(name):
    _UID[0] += 1
    return f"{name}_u{_UID[0]}"


class Ring:
    def __init__(self, K, name, shape, dt, n, psum=False):
        self.items = []
        for i in range(n):
            if psum:
                t = K.es.enter_context(K.nc.psum_tensor(un(f"{name}{i}"), shape, dt))
            else:
                t = K.es.enter_context(K.nc.sbuf_tensor(un(f"{name}{i}"), shape, dt))
            self.items.append((t, Res()))
        self.i = 0

    def next(self):
        it = self.items[self.i % len(self.items)]
        self.i += 1
        return it


class K:
    pass


def rstd_ops(P, ss, r_ss, n, scale, tag_reads=()):
    P.op("act", lambda e: e.activation(out=ss, in_=ss, func=AF.Ln, bias=K.eps_col[:, 0:1], scale=scale), reads=[r_ss], writes=[r_ss])
    P.op("act", lambda e: e.activation(out=ss, in_=ss, func=AF.Exp, scale=-0.5), reads=[r_ss], writes=[r_ss])


def load_weight_bf16(P, w_sb, r_w, w_dram, kchunks, cols, c0=0):
    for k in range(kchunks):
        P.dma("pool", w_sb[:, k, :], w_dram[k * 128:(k + 1) * 128, c0:c0 + cols], writes=[r_w])


def norm_transpose(P, xt, r_xt, gb, r_gb, HT, r_HT, ptr, rings):
    ss, r_s = rings["ss"].next()
    junk, r_j = rings["junk"].next()
    xs, r_xs = rings["xs"].next()
    P.op("dve", lambda e: e.memset(ss[:, 0:1], 0.0), writes=[r_s])
    P.op("act", lambda e: e.activation(out=junk[:], in_=xt[:], func=AF.Square, accum_out=ss[:, 0:1]), reads=[r_xt, r_s], writes=[r_j, r_s])
    rstd_ops(P, ss[:, 0:1], r_s, 1, 1.0 / D)
    P.op("dve", lambda e: e.scalar_tensor_tensor(out=xs[:], in0=xt[:], scalar=ss[:, 0:1], in1=gb[:], op0=ALU.mult, op1=ALU.mult), reads=[r_xt, r_s, r_gb], writes=[r_xs])
    for half in range(2):
        pt, r_pt = ptr.next()
        for k in range(8):
            kk = half * 8 + k
            P.op("pe", lambda e, pt=pt, k=k, kk=kk: e.transpose(out=pt[:, k * 128:(k + 1) * 128], in_=xs[:, kk * 128:(kk + 1) * 128], identity=K.ident[:]), reads=[r_xs], writes=[r_pt])
        if half == 0:
            P.op("act", lambda e, pt=pt: e.copy(out=HT[:, 0:8, :].rearrange("p k t -> p (k t)"), in_=pt[:]), reads=[r_pt], writes=[r_HT])
        else:
            P.op("dve", lambda e, pt=pt: e.tensor_copy(out=HT[:, 8:16, :].rearrange("p k t -> p (k t)"), in_=pt[:]), reads=[r_pt], writes=[r_HT])


def group_rmsnorm(P, src, r_src, dst, r_dst, ngroups, rings, width=128):
    sq, r_sq = rings["sq"].next()
    ss, r_s = rings["ss"].next()
    n = ngroups * width
    P.op("dve", lambda e: e.tensor_tensor(out=sq[:, 0:n], in0=src, in1=src, op=ALU.mult), reads=[r_src], writes=[r_sq])
    P.op("dve", lambda e: e.reduce_sum(out=ss[:, 0:ngroups], in_=sq[:, 0:n].rearrange("p (h d) -> p h d", h=ngroups), axis=AX.X), reads=[r_sq], writes=[r_s])
    rstd_ops(P, ss[:, 0:ngroups], r_s, ngroups, 1.0 / width)
    P.op("dve", lambda e: e.tensor_tensor(out=dst.rearrange("p (h d) -> p h d", h=ngroups), in0=src.rearrange("p (h d) -> p h d", h=ngroups),
                                          in1=ss[:, 0:ngroups].unsqueeze(2).to_broadcast([128, ngroups, width]), op=ALU.mult), reads=[r_src, r_s], writes=[r_dst])


def phase_proj(P, nc, which, x_dram, ntiles, w_dram, C, S):
    with ExitStack() as es:
        K.es = es

        def sb(name, shape, dt):
            return es.enter_context(nc.sbuf_tensor(un(name), shape, dt)), Res()
        W, r_W = sb("W" + which, [128, 16, C], BF16)
        load_weight_bf16(P, W, r_W, w_dram, 16, C)
        gb, r_gb = sb("gb", [128, D], F32)
        P.dma("sp", gb[:], S["g_mix"][0].partition_broadcast(128), writes=[r_gb])
        rings = {
            "ss": Ring(K, "ss", [128, 8], F32, 4),
            "junk": Ring(K, "junk", [128, D], F32, 1),
            "xs": Ring(K, "xs", [128, D], BF16, 2),
            "sq": Ring(K, "sq", [128, 512], F32, 2),
        }
        xr = Ring(K, "xt", [128, D], F32, 2)
        HTr = Ring(K, "HT", [128, 16, 128], BF16, 2)
        ptr = Ring(K, "pt", [128, 1024], BF16, 2, psum=True)
        pbr = Ring(K, "pb", [128, 512], F32, 4, psum=True)
        tmf = Ring(K, "tmf", [128, 512], F32, 3)
        tmb = Ring(K, "tmb", [128, 512], BF16, 3)
        if which == "A":
            fmo = Ring(K, "fmo", [128, 12, 128], BF16, 2)
            tmo = Ring(K, "tmo", [128, 2048], BF16, 2)
            alr = Ring(K, "alr", [128, 16], BF16, 2)
            alf = Ring(K, "alf", [16, 128], BF16, 2)
            gk, r_gk = sb("gk", [128, 2], F32)
            P.dma("sp", gk[:, 0:1], S["g_ks"].rearrange("o d -> d o"), writes=[r_gk])
            P.dma("sp", gk[:, 1:2], S["g_kw"].rearrange("o d -> d o"), writes=[r_gk])
        else:
            fmo = Ring(K, "fmo", [128, 12, 128], BF16, 2)
            rlo = Ring(K, "rlo", [128, 1024], F32, 2)
            gto = Ring(K, "gto", [128, 24], F32, 2)
            gq, r_gq = sb("gq", [128, 1], F32)
            P.dma("sp", gq[:, 0:1], S["g_q"].rearrange("o d -> d o"), writes=[r_gq])
            P.op("dve", lambda e: e.tensor_scalar(out=gq[:], in0=gq[:], scalar1=128.0 ** -0.5, scalar2=None, op0=ALU.mult), reads=[r_gq], writes=[r_gq])
            bg, r_bg = sb("bg", [128, 24], F32)
            P.dma("sp", bg[:], S["b_nsa_gate"][0].partition_broadcast(128), writes=[r_bg])

        nblk = (C + 511) // 512
        for t in range(ntiles):
            xt, r_xt = xr.next()
            P.dma("sp", xt[:], x_dram[t * 128:(t + 1) * 128, :], writes=[r_xt])
            KCUT = int(os.environ.get("KCUT", 99))
            if KCUT < 1:
                continue
            HT, r_HT = HTr.next()
            norm_transpose(P, xt, r_xt, gb, r_gb, HT, r_HT, ptr, rings)
            if KCUT < 2:
                continue
            fm, r_fm = fmo.next()
            if which == "A":
                tm, r_tm = tmo.next()
            else:
                rl, r_rl = rlo.next()
            pt, r_pt = ptr.next()
            pt2, r_pt2 = ptr.next()
            for nb in range(min(nblk, KCUT - 2)):
                c0 = nb * 512
                w = min(512, C - c0)
                pb, r_pb = pbr.next()
                for k in range(16):
                    P.op("pe", lambda e, pb=pb, k=k, c0=c0, w=w, HT=HT: e.matmul(pb[:, 0:w], lhsT=HT[:, k, :], rhs=W[:, k, c0:c0 + w], start=(k == 0), stop=(k == 15)),
                         reads=[r_HT, r_W], writes=[r_pb])
                if which == "A":
                    if nb == 0:
                        b, r_b = tmb.next()
                        P.op("act", lambda e, b=b, pb=pb: e.copy(out=b[:], in_=pb[:]), reads=[r_pb], writes=[r_b])
                        for j in range(4):
                            P.op("pe", lambda e, j=j, b=b, pt=pt: e.transpose(out=pt[:, j * 128:(j + 1) * 128], in_=b[:, j * 128:(j + 1) * 128], identity=K.ident[:]), reads=[r_b], writes=[r_pt])
                    elif nb == 1:
                        f, r_f = tmf.next()
                        P.op("act", lambda e, f=f, pb=pb: e.copy(out=f[:], in_=pb[:]), reads=[r_pb], writes=[r_f])
                        b, r_b = tmb.next()
                        group_rmsnorm(P, f[:], r_f, b[:], r_b, 4, rings)
                        for j in range(4):
                            P.op("pe", lambda e, j=j, b=b, pt=pt: e.transpose(out=pt[:, (4 + j) * 128:(5 + j) * 128], in_=b[:, j * 128:(j + 1) * 128], identity=K.ident[:]), reads=[r_b], writes=[r_pt])
                        P.op("dve", lambda e, pt=pt, fm=fm: e.tensor_copy(out=fm[:, 0:4, :].rearrange("p k t -> p (k t)"), in_=pt[:, 0:512]), reads=[r_pt], writes=[r_fm])
                        P.op("act", lambda e, pt=pt, fm=fm: e.activation(out=fm[:, 4:6, :].rearrange("p k t -> p (k t)"), in_=pt[:, 512:768], func=AF.Copy, scale=gk[:, 0:1]), reads=[r_pt, r_gk], writes=[r_fm])
                        P.op("act", lambda e, pt=pt, fm=fm: e.activation(out=fm[:, 6:8, :].rearrange("p k t -> p (k t)"), in_=pt[:, 768:1024], func=AF.Copy, scale=gk[:, 1:2]), reads=[r_pt, r_gk], writes=[r_fm])
                    elif nb == 2:
                        P.op("act", lambda e, pb=pb, tm=tm: e.copy(out=tm[:, 0:512], in_=pb[:]), reads=[r_pb], writes=[r_tm])
                    elif nb == 3:
                        P.op("dve", lambda e, pb=pb, tm=tm: e.tensor_copy(out=tm[:, 512:1024], in_=pb[:]), reads=[r_pb], writes=[r_tm])
                        for j in range(4):
                            P.op("pe", lambda e, j=j, tm=tm, pt2=pt2: e.transpose(out=pt2[:, j * 128:(j + 1) * 128], in_=tm[:, 512 + j * 128:512 + (j + 1) * 128], identity=K.ident[:]), reads=[r_tm], writes=[r_pt2])
                    elif nb in (4, 5):
                        o = 1024 + (nb - 4) * 512
                        if nb == 4:
                            P.op("act", lambda e, pb=pb, tm=tm, o=o: e.copy(out=tm[:, o:o + 512], in_=pb[:]), reads=[r_pb], writes=[r_tm])
                        else:
                            P.op("dve", lambda e, pb=pb, tm=tm, o=o: e.tensor_copy(out=tm[:, o:o + 512], in_=pb[:]), reads=[r_pb], writes=[r_tm])
                    else:
                        P.op("dve", lambda e, pt2=pt2, fm=fm: e.tensor_copy(out=fm[:, 8:12, :].rearrange("p k t -> p (k t)"), in_=pt2[:, 0:512]), reads=[r_pt2], writes=[r_fm])
                        pa_, r_pa = pbr.next()
                        for k in range(16):
                            P.op("pe", lambda e, pa_=pa_, k=k, HT=HT: e.matmul(pa_[0:16, 0:128], lhsT=W[:, k, 3072:3088], rhs=HT[:, k, :], start=(k == 0), stop=(k == 15)),
                                 reads=[r_HT, r_W], writes=[r_pa])
                        af, r_af = alf.next()
                        P.op("act", lambda e, pa_=pa_, af=af: e.copy(out=af[:], in_=pa_[0:16, 0:128]), reads=[r_pa], writes=[r_af])
                        P.dma("sp", S["alT"][:, t * 128:(t + 1) * 128], af[:], reads=[r_af], writes=[S["r_alT"]])
                else:
                    if nb in (0, 1):
                        f, r_f = tmf.next()
                        P.op("act", lambda e, f=f, pb=pb: e.copy(out=f[:], in_=pb[:]), reads=[r_pb], writes=[r_f])
                        b, r_b = tmb.next()
                        group_rmsnorm(P, f[:], r_f, b[:], r_b, 4, rings)
                        for j in range(4):
                            P.op("pe", lambda e, j=j, b=b, pt=pt, nb=nb: e.transpose(out=pt[:, (4 * nb + j) * 128:(4 * nb + j + 1) * 128], in_=b[:, j * 128:(j + 1) * 128], identity=K.ident[:]), reads=[r_b], writes=[r_pt])
                        if nb == 1:
                            P.op("act", lambda e, pt=pt, fm=fm: e.activation(out=fm[:, 0:8, :].rearrange("p k t -> p (k t)"), in_=pt[:], func=AF.Copy, scale=gq[:, 0:1]), reads=[r_pt, r_gq], writes=[r_fm])
                    elif nb == 2:
                        b, r_b = tmb.next()
                        P.op("act", lambda e, b=b, pb=pb: e.activation(out=b[:], in_=pb[:], func=AF.Copy, scale=128.0 ** -0.5), reads=[r_pb], writes=[r_b])
                        for j in range(4):
                            P.op("pe", lambda e, j=j, b=b, pt2=pt2: e.transpose(out=pt2[:, j * 128:(j + 1) * 128], in_=b[:, j * 128:(j + 1) * 128], identity=K.ident[:]), reads=[r_b], writes=[r_pt2])
                        P.op("dve", lambda e, pt2=pt2, fm=fm: e.tensor_copy(out=fm[:, 8:12, :].rearrange("p k t -> p (k t)"), in_=pt2[:, 0:512]), reads=[r_pt2], writes=[r_fm])
                    elif nb in (3, 4):
                        o = (nb - 3) * 512
                        f, r_f = tmf.next()
                        P.op("act", lambda e, f=f, pb=pb: e.activation(out=f[:], in_=pb[:], func=AF.Exp, scale=-1.0), reads=[r_pb], writes=[r_f])
                        P.op("dve", lambda e, f=f: e.tensor_scalar(out=f[:], in0=f[:], scalar1=1.0, scalar2=None, op0=ALU.add), reads=[r_f], writes=[r_f])
                        P.op("dve", lambda e, f=f: e.reciprocal(out=f[:], in_=f[:]), reads=[r_f], writes=[r_f])
                        P.op("dve", lambda e, f=f, pb=pb, rl=rl, o=o: e.tensor_tensor(out=rl[:, o:o + 512], in0=f[:], in1=pb[:], op=ALU.mult), reads=[r_f, r_pb], writes=[r_rl])
                    else:
                        gt, r_gt = gto.next()
                        P.op("dve", lambda e, pb=pb, gt=gt: e.tensor_tensor(out=gt[:], in0=pb[:, 0:24], in1=bg[:], op=ALU.add), reads=[r_pb, r_bg], writes=[r_gt])
                        P.op("act", lambda e, gt=gt: e.activation(out=gt[:], in_=gt[:], func=AF.Exp, scale=-1.0), reads=[r_gt], writes=[r_gt])
                        P.op("dve", lambda e, gt=gt: e.tensor_scalar(out=gt[:], in0=gt[:], scalar1=1.0, scalar2=None, op0=ALU.add), reads=[r_gt], writes=[r_gt])
                        P.op("dve", lambda e, gt=gt: e.reciprocal(out=gt[:], in_=gt[:]), reads=[r_gt], writes=[r_gt])
                        P.dma("sp", S["gates"][t * 128:(t + 1) * 128, :], gt[:], reads=[r_gt], writes=[S["r_gates"]])
            if which == "A":
                P.dma("sp", S["fmA"][:, :, t * 128:(t + 1) * 128].rearrange("k d t -> d k t"), fm[:], reads=[r_fm], writes=[S["r_fmA"]])
                P.dma("sp", S["tmA"][t * 128:(t + 1) * 128, :], tm[:], reads=[r_tm], writes=[S["r_tmA"]])
            else:
                P.dma("sp", S["fmB"][:, :, t * 128:(t + 1) * 128].rearrange("k d t -> d k t"), fm[:], reads=[r_fm], writes=[S["r_fmB"]])
                P.dma("sp", S["rl"][t * 128:(t + 1) * 128, :], rl[:], reads=[r_rl], writes=[S["r_rl"]])
        P.flush()


def phase_compress(P, nc, S):
    with ExitStack() as es:
        K.es = es

        def sb(name, shape, dt):
            return es.enter_context(nc.sbuf_tensor(un(name), shape, dt)), Res()
        UTr = Ring(K, "UT", [128, T], BF16, 2)
        W1, r_W1 = sb("W1", [128, 32, 256], BF16)
        W2, r_W2 = sb("W2", [128, 2, 128], BF16)
        peT, r_pe = sb("peT", [128, 32], BF16)
        c1, r_c1 = sb("c1", [128, 4], F32)
        hs, r_hs = sb("hs", [128, 2, 512], BF16)
        ef, r_ef = sb("ef", [128, 512], F32)
        gkc, r_gkc = sb("gkc", [128, 1], F32)
        P.dma("sp", gkc[:], S["g_kc"].rearrange("o d -> d o"), writes=[r_gkc])
        rings = {"ss": Ring(K, "ssc", [128, 8], F32, 2), "sq": Ring(K, "sqc", [128, 512], F32, 2)}
        of = Ring(K, "of", [128, 128], F32, 2)
        ob = Ring(K, "ob", [128, 128], BF16, 2)
        pbr = Ring(K, "pbc", [128, 512], F32, 3, psum=True)
        ptr = Ring(K, "ptc", [128, 1024], BF16, 1, psum=True)
        P.op("pool", lambda e: e.memset(hs[:], 0.0), writes=[r_hs])
        P.op("pool", lambda e: e.memset(K.kcT[:], 0.0), writes=[K.r_kcT])
        P.op("pool", lambda e: e.memset(K.vcx[:], 0.0), writes=[K.r_vcx])
        for kv in range(2):
            for c in range(4):
                P.dma("pool", K.vcx[:, kv, c, 0:127], S["Mmat"][c], writes=[K.r_vcx])
        P.op("pool", lambda e: e.memset(K.vcx[:, :, :, 255:256], 1.0), writes=[K.r_vcx])
        for which in range(2):
            w1d = S["w_ck1"] if which == 0 else S["w_cv1"]
            w2d = S["w_ck2"] if which == 0 else S["w_cv2"]
            ped = S["pe_kT"] if which == 0 else S["pe_vT"]
            for l in range(32):
                P.dma("pool", W1[:, l, :], w1d[l * 128:(l + 1) * 128, :], writes=[r_W1])
            for jh in range(2):
                P.dma("pool", W2[:, jh, :], w2d[jh * 128:(jh + 1) * 128, :], writes=[r_W2])
            P.dma("pool", peT[:], ped, writes=[r_pe])
            for jh in range(2):
                pb, r_pb = pbr.next()
                for l in range(32):
                    P.op("pe", lambda e, pb=pb, l=l, jh=jh: e.matmul(pb[:, 0:1], lhsT=W1[:, l, jh * 128:(jh + 1) * 128], rhs=peT[:, l:l + 1], start=(l == 0), stop=(l == 31)),
                         reads=[r_W1, r_pe], writes=[r_pb])
                P.op("dve", lambda e, pb=pb, jh=jh: e.tensor_copy(out=c1[:, jh:jh + 1], in_=pb[:, 0:1]), reads=[r_pb], writes=[r_c1])
                P.op("dve", lambda e, jh=jh: e.tensor_scalar(out=c1[:, 2 + jh:3 + jh], in0=c1[:, jh:jh + 1], scalar1=-1.0, scalar2=None, op0=ALU.mult), reads=[r_c1], writes=[r_c1])
            for kv in range(2):
                UT, r_UT = UTr.next()
                P.dma("sp", UT[:], S["fmA"][2 * which + kv], reads=[S["r_fmA"]], writes=[r_UT])
                for jh in range(2):
                    pb, r_pb = pbr.next()
                    for l in range(32):
                        P.op("pe", lambda e, pb=pb, l=l, jh=jh, UT=UT: e.matmul(pb[:, 0:511], lhsT=W1[:, l, jh * 128:(jh + 1) * 128], rhs=UT[:, l:l + 16 * 510 + 1:16], start=(l == 0), stop=(l == 31)),
                             reads=[r_W1, r_UT], writes=[r_pb])
                    P.op("act", lambda e, pb=pb, jh=jh: e.activation(out=ef[:, 0:511], in_=pb[:, 0:511], func=AF.Exp, scale=-1.0, bias=c1[:, 2 + jh:3 + jh]), reads=[r_pb, r_c1], writes=[r_ef])
                    P.op("dve", lambda e: e.tensor_scalar(out=ef[:, 0:511], in0=ef[:, 0:511], scalar1=1.0, scalar2=None, op0=ALU.add), reads=[r_ef], writes=[r_ef])
                    P.op("dve", lambda e: e.reciprocal(out=ef[:, 0:511], in_=ef[:, 0:511]), reads=[r_ef], writes=[r_ef])
                    P.op("dve", lambda e, pb=pb, jh=jh: e.scalar_tensor_tensor(out=hs[:, jh, 0:511], in0=pb[:, 0:511], scalar=c1[:, jh:jh + 1], in1=ef[:, 0:511], op0=ALU.add, op1=ALU.mult),
                         reads=[r_pb, r_c1, r_ef], writes=[r_hs])
                for c in range(4):
                    nn = 128 if c < 3 else 127
                    pb, r_pb = pbr.next()
                    for jh in range(2):
                        P.op("pe", lambda e, pb=pb, c=c, jh=jh, nn=nn: e.matmul(pb[0:nn, 0:128], lhsT=hs[:, jh, c * 128:c * 128 + nn], rhs=W2[:, jh, :], start=(jh == 0), stop=(jh == 1)),
                             reads=[r_hs, r_W2], writes=[r_pb])
                    if which == 0:
                        f, r_f = of.next()
                        b, r_b = ob.next()
                        if nn < 128:
                            P.op("pool", lambda e, f=f: e.memset(f[:], 1.0), writes=[r_f])
                        P.op("act", lambda e, f=f, pb=pb, nn=nn: e.copy(out=f[0:nn, :], in_=pb[0:nn, 0:128]), reads=[r_pb], writes=[r_f])
                        group_rmsnorm(P, f[:], r_f, b[:], r_b, 1, rings)
                        pt, r_pt = ptr.next()
                        P.op("pe", lambda e, b=b, pt=pt: e.transpose(out=pt[:, 0:128], in_=b[:], identity=K.ident[:]), reads=[r_b], writes=[r_pt])
                        P.op("act", lambda e, pt=pt, kv=kv, c=c, nn=nn: e.activation(out=K.kcT[:, kv, c * 128:c * 128 + nn], in_=pt[:, 0:nn], func=AF.Copy, scale=gkc[:, 0:1]), reads=[r_pt, r_gkc], writes=[K.r_kcT])
                    else:
                        P.op("act", lambda e, pb=pb, kv=kv, c=c, nn=nn: e.copy(out=K.vcx[0:nn, kv, c, 127:255], in_=pb[0:nn, 0:128]), reads=[r_pb], writes=[K.r_vcx])
        P.flush()


def phase_nsa(P, nc, S):
    with ExitStack() as es:
        K.es = es

        def sb(name, shape, dt):
            return es.enter_context(nc.sbuf_tensor(un(name), shape, dt)), Res()
        ksT, r_ksT = sb("ksT", [128, T], BF16)
        kwT, r_kwT = sb("kwT", [128, T], BF16)
        vs1, r_vs1 = sb("vs1", [128, NT, 129], BF16)
        vw1, r_vw1 = sb("vw1", [128, NT, 129], BF16)
        Ft, r_F = sb("Ft", [128, T], BF16)
        AL, r_AL = sb("AL", [3, 65 * 128], BF16)
        ALC, r_ALC = sb("ALC", [3, 112 * 128], BF16)
        AR, r_AR = sb("AR", [3, 2, 512], BF16)
        ARC, r_ARC = sb("ARC", [3, 2, 512], BF16)
        nm, r_nm = sb("nm", [128, 4, 512], BF16)
        gno, r_gno = sb("gno", [128, 128], F32)
        P.dma("sp", Ft[:], S["Ftab"], writes=[r_F])
        P.dma("sp", AL[:], S["ALtab"], writes=[r_AL])
        P.dma("sp", ALC[:], S["ALCtab"], writes=[r_ALC])
        P.dma("sp", AR[:], S["ARtab"], writes=[r_AR])
        P.dma("sp", ARC[:], S["ARCtab"], writes=[r_ARC])
        P.dma("sp", nm[:], S["nsamask"], writes=[r_nm])
        P.dma("sp", gno[:], S["g_nsa_out"][0].partition_broadcast(128), writes=[r_gno])
        P.op("pool", lambda e: e.memset(vs1[:, :, 128:129], 1.0), writes=[r_vs1])
        P.op("pool", lambda e: e.memset(vw1[:, :, 128:129], 1.0), writes=[r_vw1])
        qtr = Ring(K, "qt", [128, 4, 128], BF16, 2)
        gtr = Ring(K, "gt", [128, 24], F32, 2)
        cmr = Ring(K, "cm", [128, 2, 512], BF16, 2)
        impr = Ring(K, "imp", [128, 128], F32, 2)
        ptr_ = Ring(K, "pT", [128, 512], BF16, 3)
        msk4r = Ring(K, "msk4", [128, 4, 128], BF16, 2)
        accr = Ring(K, "acc", [128, 4, 128], F32, 2)
        smr = Ring(K, "sm", [128, 32], F32, 3)
        wkr = Ring(K, "wk", [128, 128], F32, 2)
        wkbr = Ring(K, "wkb", [128, 128], BF16, 2)
        omr = Ring(K, "om", [128, 512], BF16, 2)
        rings = {"ss": Ring(K, "ssn", [128, 8], F32, 2), "sq": Ring(K, "sqn", [128, 512], F32, 2)}
        Sr = Ring(K, "Sps", [128, 512], F32, 2, psum=True)
        Oc = Ring(K, "Oc", [128, 512], F32, 2, psum=True)
        Ow = Ring(K, "Ow", [128, 512], F32, 2, psum=True)
        Os = Ring(K, "Os", [128, 512], F32, 2, psum=True)

        def score_tile(lhsT_k, r_k, qt, r_qt, al_lhsT, al_rhs, r_al, extra):
            ps, r_ps = Sr.next()
            n_extra = len(extra)
            P.op("pe", lambda e: e.matmul(ps[:], lhsT=lhsT_k, rhs=qt[:].rearrange("p h q -> p (h q)"), start=True, stop=False), reads=[r_k, r_qt], writes=[r_ps])
            P.op("pe", lambda e: e.matmul(ps[:], lhsT=al_lhsT, rhs=al_rhs, start=False, stop=(n_extra == 0)), reads=r_al, writes=[r_ps])
            for xi, (xl, xr, xres) in enumerate(extra):
                P.op("pe", lambda e, xl=xl, xr=xr, xi=xi: e.matmul(ps[:], lhsT=xl, rhs=xr, start=False, stop=(xi == n_extra - 1)), reads=xres, writes=[r_ps])
            pT, r_pT = ptr_.next()
            P.op("act", lambda e: e.activation(out=pT[:], in_=ps[:], func=AF.Exp), reads=[r_ps], writes=[r_pT])
            return pT, r_pT

        KDCUT = int(os.environ.get("KDCUT", 99))
        for kv in range(int(os.environ.get("KDKV", 2))):
            P.dma("sp", ksT[:], S["fmA"][4 + kv], reads=[S["r_fmA"]], writes=[r_ksT])
            P.dma("sp", kwT[:], S["fmA"][6 + kv], reads=[S["r_fmA"]], writes=[r_kwT])
            for q4 in range(4):
                tsl = slice(q4 * 16, (q4 + 1) * 16)
                P.dma("sp", vs1[:, tsl, 0:128], S["tmA"][q4 * 2048:(q4 + 1) * 2048, kv * 128:(kv + 1) * 128].rearrange("(t p) d -> p t d", p=128), reads=[S["r_tmA"]], writes=[r_vs1])
                P.dma("sp", vw1[:, tsl, 0:128], S["tmA"][q4 * 2048:(q4 + 1) * 2048, 256 + kv * 128:256 + (kv + 1) * 128].rearrange("(t p) d -> p t d", p=128), reads=[S["r_tmA"]], writes=[r_vw1])
            def _tile(i, kv=kv):
                qt, r_qt = qtr.next()
                P.dma("sp", qt[:], S["fmB"][kv * 4:(kv + 1) * 4, :, i * 128:(i + 1) * 128].rearrange("h d t -> d h t"), reads=[S["r_fmB"]], writes=[r_qt])
                gt, r_gt = gtr.next()
                P.dma("sp", gt[:], S["gates"][i * 128:(i + 1) * 128, :], reads=[S["r_gates"]], writes=[r_gt])
                cm, r_cm = cmr.next()
                P.dma("sp", cm[:], S["cmask"][i], writes=[r_cm])
                imp, r_imp = impr.next()
                P.dma("sp", imp[:], S["tkmask"][i], writes=[r_imp])
                acc, r_acc = accr.next()
                sm, r_sm = smr.next()
                oc = [Oc.next(), Oc.next()]
                ow = [Ow.next(), Ow.next()]
                os_ = [Os.next(), Os.next()]
                if KDCUT < 2:
                    return
                nvis = (16 * i + 6) // 128 + 1
                for c in range(nvis):
                    extra = []
                    slot = c - (nvis - 2)
                    if slot >= 0:
                        extra.append((K.ident[:], cm[:, slot, :], [r_cm]))
                    idx = 16 * c - 2 * i + 63
                    pT, r_pT = score_tile(K.kcT[:, kv, c * 128:(c + 1) * 128], K.r_kcT, qt, r_qt,
                                          ALC[:, idx * 128:(idx + 1) * 128], ARC[:, kv, :], [r_ALC, r_ARC], extra)
                    for h in range(4):
                        o, r_o = oc[h // 2]
                        P.op("pe", lambda e, o=o, h=h, pT=pT, c=c: e.matmul(o[:, (h % 2) * 256:(h % 2 + 1) * 256], lhsT=pT[:, h * 128:(h + 1) * 128], rhs=K.vcx[:, kv, c, :], start=(c == 0 and h % 2 == 0), stop=(c == nvis - 1)),
                             reads=[r_pT, K.r_vcx], writes=[r_o])
                if KDCUT < 3:
                    return
                gcol = lambda h, br: gt[:, (4 * kv + h) * 3 + br:(4 * kv + h) * 3 + br + 1]
                for h in range(4):
                    o, r_o = oc[h // 2]
                    base = (h % 2) * 256
                    P.op("dve", lambda e, o=o, h=h, base=base: e.tensor_scalar(out=sm[:, h:h + 1], in0=o[:, base + 255:base + 256], scalar1=1e-37, scalar2=None, op0=ALU.max), reads=[r_o], writes=[r_sm])
                P.op("dve", lambda e: e.reciprocal(out=sm[:, 0:4], in_=sm[:, 0:4]), reads=[r_sm], writes=[r_sm])
                for h in range(4):
                    o, r_o = oc[h // 2]
                    base = (h % 2) * 256
                    P.op("dve", lambda e, o=o, h=h, base=base: e.scalar_tensor_tensor(out=imp[:, 0:127], in0=o[:, base:base + 127], scalar=sm[:, h:h + 1], in1=imp[:, 0:127], op0=ALU.mult, op1=ALU.add),
                         reads=[r_o, r_sm, r_imp], writes=[r_imp])
                for h in range(4):
                    P.op("dve", lambda e, h=h: e.tensor_tensor(out=sm[:, 16 + h:17 + h], in0=sm[:, h:h + 1], in1=gcol(h, 0), op=ALU.mult), reads=[r_sm, r_gt], writes=[r_sm])
                for h in range(4):
                    o, r_o = oc[h // 2]
                    base = (h % 2) * 256
                    P.op("dve", lambda e, o=o, h=h, base=base: e.tensor_scalar(out=acc[:, h, :], in0=o[:, base + 127:base + 255], scalar1=sm[:, 16 + h:17 + h], scalar2=None, op0=ALU.mult),
                         reads=[r_o, r_sm], writes=[r_acc])
                if KDCUT < 4:
                    return
                wk, r_wk = wkr.next()
                P.op("dve", lambda e: e.max(out=sm[:, 24:32], in_=imp[:]), reads=[r_imp], writes=[r_sm])
                P.op("dve", lambda e: e.match_replace(out=wk[:], in_to_replace=sm[:, 24:32], in_values=imp[:], imm_value=-3.0e38), reads=[r_imp, r_sm], writes=[r_wk])
                P.op("dve", lambda e: e.max(out=sm[:, 24:32], in_=wk[:]), reads=[r_wk], writes=[r_sm])
                P.op("dve", lambda e: e.tensor_reduce(out=sm[:, 12:13], in_=sm[:, 24:32], axis=AX.X, op=ALU.min), reads=[r_sm], writes=[r_sm])
                P.op("dve", lambda e: e.tensor_scalar(out=sm[:, 31:32], in0=sm[:, 12:13], scalar1=-1.0e29, scalar2=None, op0=ALU.max), reads=[r_sm], writes=[r_sm])
                P.op("dve", lambda e: e.tensor_scalar(out=wk[:], in0=imp[:], scalar1=sm[:, 31:32], scalar2=None, op0=ALU.is_ge), reads=[r_imp, r_sm], writes=[r_wk])
                wkb, r_wkb = wkbr.next()
                P.op("dve", lambda e: e.tensor_scalar(out=wkb[:], in0=wk[:], scalar1=-NEGM, scalar2=NEGM, op0=ALU.mult, op1=ALU.add), reads=[r_wk], writes=[r_wkb])
                psm, r_psm = Sr.next()
                P.op("pe", lambda e: e.matmul(psm[:, 0:128], lhsT=wkb[:], rhs=K.ident[:], start=True, stop=True), reads=[r_wkb], writes=[r_psm])
                msk4, r_msk4 = msk4r.next()
                P.op("dve", lambda e: e.tensor_copy(out=msk4[:], in_=psm[:, 0:128].unsqueeze(1).to_broadcast([128, 4, 128])), reads=[r_psm], writes=[r_msk4])
                if KDCUT < 5:
                    return
                wtiles = [j for j in range(2 * i - 4, 2 * i + 2) if 0 <= j < NT]
                for wi, j in enumerate(wtiles):
                    extra = []
                    mslot = {2 * i - 4: 0, 2 * i - 3: 1, 2 * i: 2, 2 * i + 1: 3}.get(j)
                    if mslot is not None:
                        extra.append((K.ident[:], nm[:, mslot, :], [r_nm]))
                    idx = j - 2 * i + 63
                    pT, r_pT = score_tile(kwT[:, j * 128:(j + 1) * 128], r_kwT, qt, r_qt,
                                          AL[:, idx * 128:(idx + 1) * 128], AR[:, kv, :], [r_AL, r_AR], extra)
                    for h in range(4):
                        o, r_o = ow[h // 2]
                        P.op("pe", lambda e, o=o, h=h, pT=pT, j=j, wi=wi: e.matmul(o[:, (h % 2) * 129:(h % 2 + 1) * 129], lhsT=pT[:, h * 128:(h + 1) * 128], rhs=vw1[:, j, :], start=(wi == 0 and h % 2 == 0), stop=(wi == len(wtiles) - 1)),
                             reads=[r_pT, r_vw1], writes=[r_o])
                if KDCUT < 6:
                    return
                nsel = 2 * i + 2
                for j in range(nsel):
                    extra = [(Ft[:, j * 128:(j + 1) * 128], msk4[:].rearrange("p h q -> p (h q)"), [r_F, r_msk4])]
                    if j >= 2 * i:
                        extra.append((K.ident[:], nm[:, 2 + (j - 2 * i), :], [r_nm]))
                    idx = j - 2 * i + 63
                    pT, r_pT = score_tile(ksT[:, j * 128:(j + 1) * 128], r_ksT, qt, r_qt,
                                          AL[:, idx * 128:(idx + 1) * 128], AR[:, kv, :], [r_AL, r_AR], extra)
                    for h in range(4):
                        o, r_o = os_[h // 2]
                        P.op("pe", lambda e, o=o, h=h, pT=pT, j=j: e.matmul(o[:, (h % 2) * 129:(h % 2 + 1) * 129], lhsT=pT[:, h * 128:(h + 1) * 128], rhs=vs1[:, j, :], start=(j == 0 and h % 2 == 0), stop=(j == nsel - 1)),
                             reads=[r_pT, r_vs1], writes=[r_o])
                if KDCUT < 7:
                    return
                for bi, (obank, br) in enumerate(((ow, 2), (os_, 1))):
                    off = 4 + 4 * bi
                    for h in range(4):
                        o, r_o = obank[h // 2]
                        base = (h % 2) * 129
                        P.op("dve", lambda e, o=o, h=h, base=base, off=off: e.tensor_scalar(out=sm[:, off + h:off + h + 1], in0=o[:, base + 128:base + 129], scalar1=1e-37, scalar2=None, op0=ALU.max), reads=[r_o], writes=[r_sm])
                    P.op("dve", lambda e, off=off: e.reciprocal(out=sm[:, off:off + 4], in_=sm[:, off:off + 4]), reads=[r_sm], writes=[r_sm])
                    for h in range(4):
                        P.op("dve", lambda e, h=h, off=off, br=br: e.tensor_tensor(out=sm[:, off + h:off + h + 1], in0=sm[:, off + h:off + h + 1], in1=gcol(h, br), op=ALU.mult), reads=[r_sm, r_gt], writes=[r_sm])
                    for h in range(4):
                        o, r_o = obank[h // 2]
                        base = (h % 2) * 129
                        P.op("dve", lambda e, o=o, h=h, base=base, off=off: e.scalar_tensor_tensor(out=acc[:, h, :], in0=o[:, base:base + 128], scalar=sm[:, off + h:off + h + 1], in1=acc[:, h, :], op0=ALU.mult, op1=ALU.add),
                             reads=[r_o, r_sm, r_acc], writes=[r_acc])
                om, r_om = omr.next()
                group_rmsnorm(P, acc[:].rearrange("p h d -> p (h d)"), r_acc, om[:], r_om, 4, rings)
                P.op("dve", lambda e, om=om: e.tensor_tensor(out=om[:].rearrange("p (h d) -> p h d", h=4), in0=om[:].rearrange("p (h d) -> p h d", h=4), in1=gno[:].unsqueeze(1).to_broadcast([128, 4, 128]), op=ALU.mult),
                     reads=[r_om, r_gno], writes=[r_om])
                P.dma("sp", S["omix"][i * 128:(i + 1) * 128, kv * 512:(kv + 1) * 512], om[:], reads=[r_om], writes=[S["r_omix"]])
            for i in range(int(os.environ.get("KDI", NTO))):
                _tile(i)
        P.flush()


def phase_gla(P, nc, S):
    with ExitStack() as es:
        K.es = es

        def sb(name, shape, dt):
            return es.enter_context(nc.sbuf_tensor(un(name), shape, dt)), Res()
        klT, r_klT = sb("klT", [128, T], BF16)
        kltm, r_kltm = sb("kltm", [128, NT, 128], BF16)
        vl, r_vl = sb("vl", [128, NT, 256], BF16)
        qlT, r_qlT = sb("qlT", [128, TO], BF16)
        alT, r_alT = sb("alT", [16, T], BF16)
        wgk, r_wgk = sb("wgk", [16, 128], BF16)
        bgk, r_bgk = sb("bgk", [128, 128], F32)
        ggo, r_ggo = sb("ggo", [128, 256], F32)
        triI, r_tri = sb("triI", [128, 128], F32)
        triA, r_triA = sb("triA", [128, 128], F32)
        cau, r_cau = sb("cau", [128, 128], F32)
        m01, r_m01 = sb("m01", [128, 2], F32)
        St, r_St = sb("St", [128, 256], F32)
        P.dma("sp", alT[:], S["alT"], reads=[S["r_alT"]], writes=[r_alT])
        P.dma("sp", ggo[:], S["g_gla_out"][0].partition_broadcast(128), writes=[r_ggo])
        P.dma("sp", triI[:], S["triI"], writes=[r_tri])
        P.dma("sp", triA[:], S["triA"], writes=[r_triA])
        P.dma("sp", cau[:], S["cau"], writes=[r_cau])
        P.dma("sp", m01[:], S["m01"], writes=[r_m01])
        Sbr = Ring(K, "Sb", [128, 256], BF16, 2)
        lar = Ring(K, "la", [128, 128], F32, 2)
        Ebr = Ring(K, "Eb", [128, 128], F32, 2)
        Enr = Ring(K, "En", [128, 128], F32, 2)
        Ear = Ring(K, "Ea", [128, 128], F32, 2)
        Ktr = Ring(K, "Kt", [128, 128], BF16, 2)
        Khr = Ring(K, "Kh", [128, 128], BF16, 2)
        Qtr = Ring(K, "Qt", [128, 128], BF16, 2)
        Atr = Ring(K, "At", [128, 128], BF16, 2)
        rlr = Ring(K, "rlt", [128, 256], F32, 2)
        ofr = Ring(K, "ofg", [128, 256], F32, 2)
        obr = Ring(K, "obg", [128, 256], BF16, 2)
        rings = {"ss": Ring(K, "ssg", [128, 8], F32, 2), "sq": Ring(K, "sqg", [128, 512], F32, 2)}
        pz = Ring(K, "pz", [128, 512], F32, 2, psum=True)
        pa = Ring(K, "pa", [128, 512], F32, 2, psum=True)
        pkv = Ring(K, "pkv", [128, 512], F32, 2, psum=True)
        po = Ring(K, "po", [128, 512], F32, 2, psum=True)

        for hl in range(4):
            P.dma("sp", klT[:], S["fmA"][8 + hl], reads=[S["r_fmA"]], writes=[r_klT])
            P.dma("sp", qlT[:], S["fmB"][8 + hl], reads=[S["r_fmB"]], writes=[r_qlT])
            for q4 in range(4):
                tsl = slice(q4 * 16, (q4 + 1) * 16)
                P.dma("sp", kltm[:, tsl, :], S["tmA"][q4 * 2048:(q4 + 1) * 2048, 512 + hl * 128:512 + (hl + 1) * 128].rearrange("(t p) d -> p t d", p=128), reads=[S["r_tmA"]], writes=[r_kltm])
                P.dma("sp", vl[:, tsl, :], S["tmA"][q4 * 2048:(q4 + 1) * 2048, 1024 + hl * 256:1024 + (hl + 1) * 256].rearrange("(t p) d -> p t d", p=128), reads=[S["r_tmA"]], writes=[r_vl])
            P.dma("pool", wgk[:], S["w_gk2"][:, hl * 128:(hl + 1) * 128], writes=[r_wgk])
            P.dma("sp", bgk[:], S["b_gk"][0, hl * 128:(hl + 1) * 128].partition_broadcast(128), writes=[r_bgk])
            P.op("pool", lambda e: e.memset(St[:], 0.0), writes=[r_St])
            Sb, r_Sb = Sbr.next()
            P.op("pool", lambda e, Sb=Sb: e.memset(Sb[:], 0.0), writes=[r_Sb])
            Sbox = [(Sb, r_Sb)]
            def _pair(i, hl=hl):
                nonlocal_Sb = Sbox
                Sb, r_Sb = Sbox[0]
                po_, r_po = po.next()
                rlt, r_rlt = rlr.next()
                P.dma("sp", rlt[:], S["rl"][i * 128:(i + 1) * 128, hl * 256:(hl + 1) * 256], reads=[S["r_rl"]], writes=[r_rlt])
                for X in range(2):
                    c = 2 * i + X
                    z, r_z = pz.next()
                    P.op("pe", lambda e, z=z, c=c: e.matmul(z[:, 0:128], lhsT=alT[:, c * 128:(c + 1) * 128], rhs=wgk[:], start=True, stop=True), reads=[r_alT, r_wgk], writes=[r_z])
                    la, r_la = lar.next()
                    P.op("dve", lambda e, z=z, la=la: e.tensor_tensor(out=la[:], in0=z[:, 0:128], in1=bgk[:], op=ALU.add), reads=[r_z, r_bgk], writes=[r_la])
                    P.op("act", lambda e, la=la: e.activation(out=la[:], in_=la[:], func=AF.Exp, scale=-1.0), reads=[r_la], writes=[r_la])
                    P.op("act", lambda e, la=la: e.activation(out=la[:], in_=la[:], func=AF.Ln, bias=K.one_col[:, 0:1], scale=1.0), reads=[r_la], writes=[r_la])
                    zb, r_zb = pz.next()
                    P.op("pe", lambda e, zb=zb, la=la: e.matmul(zb[:, 0:128], lhsT=la[:], rhs=triI[:], start=True, stop=True), reads=[r_la, r_tri], writes=[r_zb])
                    P.op("pe", lambda e, zb=zb, la=la: e.matmul(zb[:, 128:256], lhsT=triA[:], rhs=la[:], start=True, stop=True), reads=[r_la, r_triA], writes=[r_zb])
                    Eb, r_Eb = Ebr.next()
                    En, r_En = Enr.next()
                    Ea, r_Ea = Ear.next()
                    P.op("act", lambda e, zb=zb, Eb=Eb: e.activation(out=Eb[:], in_=zb[:, 0:128], func=AF.Exp), reads=[r_zb], writes=[r_Eb])
                    P.op("act", lambda e, zb=zb, En=En: e.activation(out=En[:], in_=zb[:, 0:128], func=AF.Exp, scale=-1.0), reads=[r_zb], writes=[r_En])
                    P.op("act", lambda e, zb=zb, Ea=Ea: e.activation(out=Ea[:], in_=zb[:, 128:256], func=AF.Exp), reads=[r_zb], writes=[r_Ea])
                    Kt, r_Kt = Ktr.next()
                    Kh, r_Kh = Khr.next()
                    Qt, r_Qt = Qtr.next()
                    P.op("dve", lambda e, Kt=Kt, En=En, c=c: e.tensor_tensor(out=Kt[:], in0=klT[:, c * 128:(c + 1) * 128], in1=En[:], op=ALU.mult), reads=[r_klT, r_En], writes=[r_Kt])
                    P.op("dve", lambda e, Kh=Kh, Ea=Ea, c=c: e.tensor_tensor(out=Kh[:], in0=kltm[:, c, :], in1=Ea[:], op=ALU.mult), reads=[r_kltm, r_Ea], writes=[r_Kh])
                    P.op("dve", lambda e, Qt=Qt, Eb=Eb, i=i, X=X: e.scalar_tensor_tensor(out=Qt[:], in0=qlT[:, i * 128:(i + 1) * 128], scalar=m01[:, X:X + 1], in1=Eb[:], op0=ALU.mult, op1=ALU.mult),
                         reads=[r_qlT, r_m01, r_Eb], writes=[r_Qt])
                    a_, r_a = pa.next()
                    P.op("pe", lambda e, a_=a_, Kt=Kt, Qt=Qt: e.matmul(a_[:, 0:128], lhsT=Kt[:], rhs=Qt[:], start=True, stop=True), reads=[r_Kt, r_Qt], writes=[r_a])
                    At, r_At = Atr.next()
                    P.op("dve", lambda e, a_=a_, At=At: e.tensor_tensor(out=At[:], in0=a_[:, 0:128], in1=cau[:], op=ALU.mult), reads=[r_a, r_cau], writes=[r_At])
                    P.op("pe", lambda e, At=At, c=c, X=X, po_=po_: e.matmul(po_[:, 0:256], lhsT=At[:], rhs=vl[:, c, :], start=(X == 0), stop=False), reads=[r_At, r_vl], writes=[r_po])
                    P.op("pe", lambda e, Qt=Qt, Sb=Sb, X=X, po_=po_: e.matmul(po_[:, 0:256], lhsT=Qt[:], rhs=Sb[:], start=False, stop=(X == 1)), reads=[r_Qt, r_Sb], writes=[r_po])
                    kvp, r_kvp = pkv.next()
                    P.op("pe", lambda e, kvp=kvp, Kh=Kh, c=c: e.matmul(kvp[:, 0:256], lhsT=Kh[:], rhs=vl[:, c, :], start=True, stop=True), reads=[r_Kh, r_vl], writes=[r_kvp])
                    P.op("dve", lambda e, kvp=kvp, Eb=Eb: e.scalar_tensor_tensor(out=St[:], in0=St[:], scalar=Eb[:, 127:128], in1=kvp[:, 0:256], op0=ALU.mult, op1=ALU.add), reads=[r_St, r_Eb, r_kvp], writes=[r_St])
                    Sb, r_Sb = Sbr.next()
                    P.op("act", lambda e, Sb=Sb: e.copy(out=Sb[:], in_=St[:]), reads=[r_St], writes=[r_Sb])
                    Sbox[0] = (Sb, r_Sb)
                of_, r_of = ofr.next()
                ob_, r_ob = obr.next()
                P.op("act", lambda e, of_=of_, po_=po_: e.copy(out=of_[:], in_=po_[:, 0:256]), reads=[r_po], writes=[r_of])
                ss, r_s = rings["ss"].next()
                sq, r_sq = rings["sq"].next()
                P.op("dve", lambda e, sq=sq, of_=of_: e.tensor_tensor(out=sq[:, 0:256], in0=of_[:], in1=of_[:], op=ALU.mult), reads=[r_of], writes=[r_sq])
                P.op("dve", lambda e, sq=sq, ss=ss: e.reduce_sum(out=ss[:, 0:1], in_=sq[:, 0:256], axis=AX.X), reads=[r_sq], writes=[r_s])
                rstd_ops(P, ss[:, 0:1], r_s, 1, 1.0 / 256)
                P.op("dve", lambda e, of_=of_, ss=ss: e.scalar_tensor_tensor(out=of_[:], in0=of_[:], scalar=ss[:, 0:1], in1=ggo[:], op0=ALU.mult, op1=ALU.mult), reads=[r_of, r_s, r_ggo], writes=[r_of])
                P.op("dve", lambda e, of_=of_, ob_=ob_, rlt=rlt: e.tensor_tensor(out=ob_[:], in0=of_[:], in1=rlt[:], op=ALU.mult), reads=[r_of, r_rlt], writes=[r_ob])
                P.dma("sp", S["omix"][i * 128:(i + 1) * 128, 1024 + hl * 256:1024 + (hl + 1) * 256], ob_[:], reads=[r_ob], writes=[S["r_omix"]])
            for i in range(NTO):
                _pair(i)
        P.flush()


def phase_tail1(P, nc, S):
    with ExitStack() as es:
        K.es = es

        def sb(name, shape, dt):
            return es.enter_context(nc.sbuf_tensor(un(name), shape, dt)), Res()
        Wcq, r_Wcq = sb("Wcq", [128, 16, 512], BF16)
        Wco, r_Wco = sb("Wco", [128, 4, D], BF16)
        load_weight_bf16(P, Wcq, r_Wcq, S["w_cq"], 16, 512)
        load_weight_bf16(P, Wco, r_Wco, S["w_co"], 4, D)
        gbs = {}
        for nm_ in ("g_cross", "g_mem", "g_ffn"):
            gbs[nm_] = sb("gb_" + nm_, [128, D], F32)
            P.dma("sp", gbs[nm_][0][:], S[nm_][0].partition_broadcast(128), writes=[gbs[nm_][1]])
        gc, r_gc = sb("gc", [128, 2], F32)
        P.dma("sp", gc[:, 0:1], S["g_cq"].rearrange("o d -> d o"), writes=[r_gc])
        P.dma("sp", gc[:, 1:2], S["g_ck"].rearrange("o d -> d o"), writes=[r_gc])
        P.op("dve", lambda e: e.tensor_scalar(out=gc[:, 0:1], in0=gc[:, 0:1], scalar1=128.0 ** -0.5, scalar2=None, op0=ALU.mult), reads=[r_gc], writes=[r_gc])
        CKT, r_CKT = sb("CKT", [128, 4, 256], BF16)
        CV1, r_CV1 = sb("CV1", [128, 2, 4, 129], BF16)
        P.op("pool", lambda e: e.memset(CV1[:, :, :, 128:129], 1.0), writes=[r_CV1])
        rings = {
            "ss": Ring(K, "sst", [128, 8], F32, 4),
            "junk": Ring(K, "junkt", [128, D], F32, 1),
            "xs": Ring(K, "xst", [128, D], BF16, 2),
            "sq": Ring(K, "sqt", [128, 512], F32, 2),
        }
        xr = Ring(K, "xtt", [128, D], F32, 2)
        omr = Ring(K, "omt", [128, D], BF16, 2)
        HTr = Ring(K, "HTt", [128, 16, 128], BF16, 2)
        tmf = Ring(K, "tmft", [128, 512], F32, 2)
        tmb = Ring(K, "tmbt", [128, 512], BF16, 2)
        CQr = Ring(K, "CQT", [128, 4, 128], BF16, 2)
        pTr = Ring(K, "pTt", [128, 2, 512], BF16, 2)
        smr = Ring(K, "smt", [128, 8], F32, 2)
        ptr = Ring(K, "ptt", [128, 1024], BF16, 2, psum=True)
        pbr = Ring(K, "pbt", [128, 512], F32, 4, psum=True)

        es2 = ExitStack()
        Wkv = es2.enter_context(nc.sbuf_tensor(un("Wkv"), [128, 16, 1024], BF16))
        r_Wkv = Res()
        for k in range(16):
            P.dma("pool", Wkv[:, k, 0:512], S["w_ck"][k * 128:(k + 1) * 128, :], writes=[r_Wkv])
            P.dma("pool", Wkv[:, k, 512:1024], S["w_cv"][k * 128:(k + 1) * 128, :], writes=[r_Wkv])
        for mt in range(2):
            xt, r_xt = xr.next()
            P.dma("sp", xt[:], S["mem"][mt * 128:(mt + 1) * 128, :], writes=[r_xt])
            HT, r_HT = HTr.next()
            norm_transpose(P, xt, r_xt, gbs["g_mem"][0], gbs["g_mem"][1], HT, r_HT, ptr, rings)
            for nb in range(2):
                pb, r_pb = pbr.next()
                for k in range(16):
                    P.op("pe", lambda e, pb=pb, k=k, nb=nb, HT=HT: e.matmul(pb[:], lhsT=HT[:, k, :], rhs=Wkv[:, k, nb * 512:(nb + 1) * 512], start=(k == 0), stop=(k == 15)), reads=[r_HT, r_Wkv], writes=[r_pb])
                if nb == 0:
                    f, r_f = tmf.next()
                    b, r_b = tmb.next()
                    P.op("act", lambda e, f=f, pb=pb: e.copy(out=f[:], in_=pb[:]), reads=[r_pb], writes=[r_f])
                    group_rmsnorm(P, f[:], r_f, b[:], r_b, 4, rings)
                    pt, r_pt = ptr.next()
                    for j in range(4):
                        P.op("pe", lambda e, j=j, b=b, pt=pt: e.transpose(out=pt[:, j * 128:(j + 1) * 128], in_=b[:, j * 128:(j + 1) * 128], identity=K.ident[:]), reads=[r_b], writes=[r_pt])
                    P.op("act", lambda e, pt=pt, mt=mt: e.activation(out=CKT[:, :, mt * 128:(mt + 1) * 128], in_=pt[:, 0:512].rearrange("p (h m) -> p h m", h=4), func=AF.Copy, scale=gc[:, 1:2]), reads=[r_pt, r_gc], writes=[r_CKT])
                else:
                    P.op("act", lambda e, pb=pb, mt=mt: e.copy(out=CV1[:, mt, :, 0:128], in_=pb[:].rearrange("p (h d) -> p h d", h=4)), reads=[r_pb], writes=[r_CV1])

        P.flush()
        es2.close()
        Wout, r_Wout = sb("Wout", [128, 16, D], BF16)
        load_weight_bf16(P, Wout, r_Wout, S["w_out"], 16, D)
        for i in range(NTO):
            xt, r_xt = xr.next()
            P.dma("sp", xt[:], S["x_own"][i * 128:(i + 1) * 128, :], writes=[r_xt])
            om, r_om = omr.next()
            P.dma("sp", om[:], S["omix"][i * 128:(i + 1) * 128, :], reads=[S["r_omix"]], writes=[r_om])
            OT, r_OT = HTr.next()
            for half in range(2):
                pt, r_pt = ptr.next()
                for k in range(8):
                    kk = half * 8 + k
                    P.op("pe", lambda e, pt=pt, k=k, kk=kk, om=om: e.transpose(out=pt[:, k * 128:(k + 1) * 128], in_=om[:, kk * 128:(kk + 1) * 128], identity=K.ident[:]), reads=[r_om], writes=[r_pt])
                if half == 0:
                    P.op("act", lambda e, pt=pt, OT=OT: e.copy(out=OT[:, 0:8, :].rearrange("p k t -> p (k t)"), in_=pt[:]), reads=[r_pt], writes=[r_OT])
                else:
                    P.op("dve", lambda e, pt=pt, OT=OT: e.tensor_copy(out=OT[:, 8:16, :].rearrange("p k t -> p (k t)"), in_=pt[:]), reads=[r_pt], writes=[r_OT])
            for cb in range(4):
                pb, r_pb = pbr.next()
                for k in range(16):
                    P.op("pe", lambda e, pb=pb, k=k, cb=cb, OT=OT: e.matmul(pb[:], lhsT=OT[:, k, :], rhs=Wout[:, k, cb * 512:(cb + 1) * 512], start=(k == 0), stop=(k == 15)), reads=[r_OT, r_Wout], writes=[r_pb])
                P.op("dve", lambda e, pb=pb, cb=cb, xt=xt: e.tensor_tensor(out=xt[:, cb * 512:(cb + 1) * 512], in0=xt[:, cb * 512:(cb + 1) * 512], in1=pb[:], op=ALU.add), reads=[r_pb, r_xt], writes=[r_xt])
            HT, r_HT = HTr.next()
            norm_transpose(P, xt, r_xt, gbs["g_cross"][0], gbs["g_cross"][1], HT, r_HT, ptr, rings)
            pb, r_pb = pbr.next()
            for k in range(16):
                P.op("pe", lambda e, pb=pb, k=k, HT=HT: e.matmul(pb[:], lhsT=HT[:, k, :], rhs=Wcq[:, k, :], start=(k == 0), stop=(k == 15)), reads=[r_HT, r_Wcq], writes=[r_pb])
            f, r_f = tmf.next()
            b, r_b = tmb.next()
            P.op("act", lambda e, f=f, pb=pb: e.copy(out=f[:], in_=pb[:]), reads=[r_pb], writes=[r_f])
            group_rmsnorm(P, f[:], r_f, b[:], r_b, 4, rings)
            pt, r_pt = ptr.next()
            for j in range(4):
                P.op("pe", lambda e, j=j, b=b, pt=pt: e.transpose(out=pt[:, j * 128:(j + 1) * 128], in_=b[:, j * 128:(j + 1) * 128], identity=K.ident[:]), reads=[r_b], writes=[r_pt])
            CQ, r_CQ = CQr.next()
            P.op("act", lambda e, pt=pt, CQ=CQ: e.activation(out=CQ[:].rearrange("p h t -> p (h t)"), in_=pt[:, 0:512], func=AF.Copy, scale=gc[:, 0:1]), reads=[r_pt, r_gc], writes=[r_CQ])
            pT, r_pT = pTr.next()
            for mc in range(2):
                ps, r_ps = pbr.next()
                for h in range(4):
                    P.op("pe", lambda e, ps=ps, h=h, mc=mc, CQ=CQ: e.matmul(ps[:, h * 128:(h + 1) * 128], lhsT=CKT[:, h, mc * 128:(mc + 1) * 128], rhs=CQ[:, h, :], start=True, stop=True), reads=[r_CKT, r_CQ], writes=[r_ps])
                P.op("act", lambda e, ps=ps, pT=pT, mc=mc: e.activation(out=pT[:, mc, :], in_=ps[:], func=AF.Exp), reads=[r_ps], writes=[r_pT])
            po0, r_po0 = pbr.next()
            po1, r_po1 = pbr.next()
            pos = [(po0, r_po0), (po1, r_po1)]
            for h in range(4):
                o, r_o = pos[h // 2]
                for mc in range(2):
                    P.op("pe", lambda e, o=o, h=h, mc=mc, pT=pT: e.matmul(o[:, (h % 2) * 129:(h % 2 + 1) * 129], lhsT=pT[:, mc, h * 128:(h + 1) * 128], rhs=CV1[:, mc, h, :], start=(mc == 0), stop=(mc == 1)), reads=[r_pT, r_CV1], writes=[r_o])
            sm, r_sm = smr.next()
            for h in range(4):
                o, r_o = pos[h // 2]
                base = (h % 2) * 129
                P.op("dve", lambda e, o=o, h=h, base=base, sm=sm: e.tensor_copy(out=sm[:, h:h + 1], in_=o[:, base + 128:base + 129]), reads=[r_o], writes=[r_sm])
            P.op("dve", lambda e, sm=sm: e.reciprocal(out=sm[:, 0:4], in_=sm[:, 0:4]), reads=[r_sm], writes=[r_sm])
            cab, r_cab = tmb.next()
            for h in range(4):
                o, r_o = pos[h // 2]
                base = (h % 2) * 129
                P.op("dve", lambda e, o=o, h=h, base=base, sm=sm, cab=cab: e.tensor_scalar(out=cab[:, h * 128:(h + 1) * 128], in0=o[:, base:base + 128], scalar1=sm[:, h:h + 1], scalar2=None, op0=ALU.mult), reads=[r_o, r_sm], writes=[r_cab])
            pt, r_pt = ptr.next()
            for j in range(4):
                P.op("pe", lambda e, j=j, cab=cab, pt=pt: e.transpose(out=pt[:, j * 128:(j + 1) * 128], in_=cab[:, j * 128:(j + 1) * 128], identity=K.ident[:]), reads=[r_cab], writes=[r_pt])
            CA, r_CA = CQr.next()
            P.op("act", lambda e, pt=pt, CA=CA: e.copy(out=CA[:].rearrange("p h t -> p (h t)"), in_=pt[:, 0:512]), reads=[r_pt], writes=[r_CA])
            for cb in range(4):
                pb, r_pb = pbr.next()
                for k in range(4):
                    P.op("pe", lambda e, pb=pb, k=k, cb=cb, CA=CA: e.matmul(pb[:], lhsT=CA[:, k, :], rhs=Wco[:, k, cb * 512:(cb + 1) * 512], start=(k == 0), stop=(k == 3)), reads=[r_CA, r_Wco], writes=[r_pb])
                P.op("dve", lambda e, pb=pb, cb=cb, xt=xt: e.tensor_tensor(out=xt[:, cb * 512:(cb + 1) * 512], in0=xt[:, cb * 512:(cb + 1) * 512], in1=pb[:], op=ALU.add), reads=[r_pb, r_xt], writes=[r_xt])
            P.dma("sp", S["x2"][i * 128:(i + 1) * 128, :], xt[:], reads=[r_xt], writes=[S["r_x2"]])
            HF, r_HF = HTr.next()
            norm_transpose(P, xt, r_xt, gbs["g_ffn"][0], gbs["g_ffn"][1], HF, r_HF, ptr, rings)
            P.dma("sp", S["hfT"][:, :, i * 128:(i + 1) * 128].rearrange("k d t -> d k t"), HF[:], reads=[r_HF], writes=[S["r_hfT"]])
        P.flush()


def phase_ffn(P, nc, S):
    NHB = DFF // 128
    with ExitStack() as es:
        K.es = es
        HFr = Ring(K, "HFm", [128, 16, 512], BF16, 1)
        ACT_ = Ring(K, "ACTT", [128, NHB, 512], BF16, 1)
        wgr = Ring(K, "wg", [128, 16, 256], BF16, 3)
        wdr = Ring(K, "wd", [128, D], BF16, 4)
        efr = Ring(K, "eff", [128, 512], F32, 2)
        x2r = Ring(K, "x2t", [128, D], F32, 2)
        pg = Ring(K, "pg", [128, 512], F32, 4, psum=True)
        pdn = Ring(K, "pdn", [128, 512], F32, 4, psum=True)
        for m in range(TO // 512):
            HF, r_HF = HFr.next()
            P.dma("sp", HF[:], S["hfT"][:, :, m * 512:(m + 1) * 512].rearrange("k d t -> d k t"), reads=[S["r_hfT"]], writes=[r_HF])
            AT, r_AT = ACT_.next()
            for hb in range(NHB):
                wg, r_wg = wgr.next()
                for k in range(16):
                    P.dma("pool", wg[:, k, 0:128], S["w_gu"][k * 128:(k + 1) * 128, hb * 128:(hb + 1) * 128], writes=[r_wg])
                    P.dma("pool", wg[:, k, 128:256], S["w_gu"][k * 128:(k + 1) * 128, DFF + hb * 128:DFF + (hb + 1) * 128], writes=[r_wg])
                g_, r_g = pg.next()
                u_, r_u = pg.next()
                for k in range(16):
                    P.op("pe", lambda e, g_=g_, k=k, wg=wg, HF=HF: e.matmul(g_[:], lhsT=wg[:, k, 0:128], rhs=HF[:, k, :], start=(k == 0), stop=(k == 15)), reads=[r_wg, r_HF], writes=[r_g])
                for k in range(16):
                    P.op("pe", lambda e, u_=u_, k=k, wg=wg, HF=HF: e.matmul(u_[:], lhsT=wg[:, k, 128:256], rhs=HF[:, k, :], start=(k == 0), stop=(k == 15)), reads=[r_wg, r_HF], writes=[r_u])
                ef, r_ef = efr.next()
                P.op("act", lambda e, ef=ef, g_=g_: e.activation(out=ef[:], in_=g_[:], func=AF.Exp, scale=-1.0), reads=[r_g], writes=[r_ef])
                P.op("dve", lambda e, ef=ef: e.tensor_scalar(out=ef[:], in0=ef[:], scalar1=1.0, scalar2=None, op0=ALU.add), reads=[r_ef], writes=[r_ef])
                P.op("dve", lambda e, ef=ef: e.reciprocal(out=ef[:], in_=ef[:]), reads=[r_ef], writes=[r_ef])
                P.op("dve", lambda e, ef=ef, g_=g_: e.tensor_tensor(out=ef[:], in0=ef[:], in1=g_[:], op=ALU.mult), reads=[r_ef, r_g], writes=[r_ef])
                P.op("dve", lambda e, ef=ef, u_=u_, AT=AT, hb=hb: e.tensor_tensor(out=AT[:, hb, :], in0=ef[:], in1=u_[:], op=ALU.mult), reads=[r_ef, r_u], writes=[r_AT])
            for sp2 in range(2):
                banks = [pdn.next() for _ in range(8)] if False else None
                for sub in (2 * sp2, 2 * sp2 + 1):
                    pass
            for sub in range(4):
                i = m * 4 + sub
                x2, r_x2 = x2r.next()
                P.dma("sp", x2[:], S["x2"][i * 128:(i + 1) * 128, :], reads=[S["r_x2"]], writes=[r_x2])
                bk = [pdn.next() for _ in range(4)]
                for hb in range(NHB):
                    wd, r_wd = wdr.next()
                    P.dma("pool", wd[:], S["w_down"][hb * 128:(hb + 1) * 128, :], writes=[r_wd])
                    for cb in range(4):
                        P.op("pe", lambda e, cb=cb, hb=hb, wd=wd, AT=AT, sub=sub, bk=bk: e.matmul(bk[cb][0][:], lhsT=AT[:, hb, sub * 128:(sub + 1) * 128], rhs=wd[:, cb * 512:(cb + 1) * 512], start=(hb == 0), stop=(hb == NHB - 1)),
                             reads=[r_AT, r_wd], writes=[bk[cb][1]])
                for cb in range(4):
                    P.op("dve", lambda e, cb=cb, x2=x2, bk=bk: e.tensor_tensor(out=x2[:, cb * 512:(cb + 1) * 512], in0=x2[:, cb * 512:(cb + 1) * 512], in1=bk[cb][0][:], op=ALU.add), reads=[bk[cb][1], r_x2], writes=[r_x2])
                P.dma("sp", S["out"][i * 128:(i + 1) * 128, :], x2[:], reads=[r_x2])
        P.flush()


CA_COLS = 3088
CB_COLS = 2584

IN_SPECS = {
    "x_all": ([T, D], F32), "x_own": ([TO, D], F32), "mem": ([256, D], F32),
    "w_kv": ([D, CA_COLS], F32), "w_q": ([D, CB_COLS], F32),
    "g_mix": ([1, D], F32), "b_nsa_gate": ([1, 24], F32),
    "g_q": ([1, 128], F32), "g_kc": ([1, 128], F32), "g_ks": ([1, 128], F32), "g_kw": ([1, 128], F32),
    "pe_kT": ([128, 32], F32), "pe_vT": ([128, 32], F32),
    "w_ck1": ([4096, 256], F32), "w_ck2": ([256, 128], F32), "w_cv1": ([4096, 256], F32), "w_cv2": ([256, 128], F32),
    "g_nsa_out": ([1, 128], F32), "w_gk2": ([16, 512], F32), "b_gk": ([1, 512], F32), "g_gla_out": ([1, 256], F32),
    "w_out": ([D, D], F32), "g_cross": ([1, D], F32), "g_mem": ([1, D], F32),
    "w_cq": ([D, 512], F32), "w_ck": ([D, 512], F32), "w_cv": ([D, 512], F32),
    "g_cq": ([1, 128], F32), "g_ck": ([1, 128], F32), "w_co": ([512, D], F32), "g_ffn": ([1, D], F32),
    "w_gu": ([D, 2 * DFF], F32), "w_down": ([DFF, D], F32),
    "identb": ([128, 128], BF16), "identf": ([128, 128], F32),
    "Ftab": ([128, T], BF16), "ALtab": ([3, 65 * 128], BF16), "ALCtab": ([3, 112 * 128], BF16),
    "ARtab": ([3, 2, 512], BF16), "ARCtab": ([3, 2, 512], BF16), "nsamask": ([128, 4, 512], BF16),
    "cmask": ([NTO, 128, 2, 512], BF16), "tkmask": ([NTO, 128, 128], F32), "Mmat": ([4, 128, 127], BF16),
    "triI": ([128, 128], F32), "triA": ([128, 128], F32), "cau": ([128, 128], F32), "m01": ([128, 2], F32),
}


def build_program(phases="ABCDEFG", debug=()):
    nc = bass.Bass("TRN2", target_bir_lowering=False)
    class LazyS(dict):
        def __missing__(self, name):
            shape, dt = IN_SPECS[name]
            v = nc.dram_tensor(name, shape, dt, kind="ExternalInput").ap()
            self[name] = v
            return v
    S = LazyS()
    if phases == "ABCDEFG":
        for name in IN_SPECS:
            S[name]
    S["out"] = nc.dram_tensor("out", [TO, D], F32, kind="ExternalOutput").ap()

    def scratch(name, shape, dt):
        if name in debug:
            S[name] = nc.dram_tensor(name, shape, dt, kind="ExternalOutput").ap()
        else:
            S[name] = nc.dram_tensor(name, shape, dt).ap()
        S["r_" + name] = Res()
    scratch("fmA", [12, 128, T], BF16)
    scratch("tmA", [T, 2048], BF16)
    scratch("alT", [16, T], BF16)
    scratch("fmB", [12, 128, TO], BF16)
    scratch("rl", [TO, 1024], F32)
    scratch("gates", [TO, 24], F32)
    scratch("omix", [TO, D], BF16)
    scratch("x2", [TO, D], F32)
    scratch("hfT", [16, 128, TO], BF16)
    K.nc = nc
    with ExitStack() as es:
        sems = {e: es.enter_context(nc.semaphore("s_" + e)) for e in COMPUTE}
        dsems = {q: [es.enter_context(nc.semaphore(f"d_{q}{i}")) for i in range(DMA_RING)] for q in QUEUES}
        P = Prog(nc, sems, dsems)
        K.ident = es.enter_context(nc.sbuf_tensor("ident_sb", [128, 128], BF16))
        K.identf = es.enter_context(nc.sbuf_tensor("identf_sb", [128, 128], F32))
        K.eps_col = es.enter_context(nc.sbuf_tensor("eps_col", [128, 1], F32))
        K.one_col = es.enter_context(nc.sbuf_tensor("one_col", [128, 1], F32))
        K.kcT = es.enter_context(nc.sbuf_tensor("kcT", [128, 2, 512], BF16))
        K.r_kcT = Res()
        K.vcx = es.enter_context(nc.sbuf_tensor("vcx", [128, 2, 4, 256], BF16))
        K.r_vcx = Res()
        r_c = Res()
        P.dma("sp", K.ident[:], S["identb"], writes=[r_c])
        P.dma("sp", K.identf[:], S["identf"], writes=[r_c])
        P.op("pool", lambda e: e.memset(K.eps_col[:], EPS), writes=[r_c])
        P.op("pool", lambda e: e.memset(K.one_col[:], 1.0), writes=[r_c])
        P.flush()
        if "A" in phases:
            phase_proj(P, nc, "A", S["x_all"], int(os.environ.get("KNT", NT)), S["w_kv"], CA_COLS, S)
        if "B" in phases:
            phase_proj(P, nc, "B", S["x_own"], NTO, S["w_q"], CB_COLS, S)
        if "C" in phases:
            phase_compress(P, nc, S)
        if "D" in phases:
            phase_nsa(P, nc, S)
        if "E" in phases:
            phase_gla(P, nc, S)
        if "F" in phases:
            phase_tail1(P, nc, S)
        if "G" in phases:
            phase_ffn(P, nc, S)
    return nc


def _bf(a):
    return np.ascontiguousarray(np.asarray(a, np.float32).astype(ml_dtypes.bfloat16))


def const_tables(s):
    c = {}
    c["identb"] = _bf(np.eye(128))
    c["identf"] = np.eye(128, dtype=np.float32)
    F = np.zeros((128, T), np.float32)
    pos = np.arange(T)
    F[pos // 64, pos] = 1.0
    c["Ftab"] = _bf(F)
    kl = np.arange(128, dtype=np.float32)
    AL = np.zeros((3, 65, 128), np.float32)
    for idx in range(65):
        AL[0, idx, :] = idx - 63 - s
        AL[1, idx, :] = kl
        AL[2, idx, :] = 1.0
    c["ALtab"] = _bf(AL.reshape(3, -1))
    ALC = np.zeros((3, 112, 128), np.float32)
    for idx in range(112):
        ALC[0, idx, :] = idx - 63 - s
        ALC[1, idx, :] = kl
        ALC[2, idx, :] = 1.0
    c["ALCtab"] = _bf(ALC.reshape(3, -1))
    AR = np.zeros((3, 2, 4, 128), np.float32)
    ARC = np.zeros((3, 2, 4, 128), np.float32)
    for kv in range(2):
        for h in range(4):
            sl = 2.0 ** -(4 * kv + h + 1)
            AR[0, kv, h, :] = 128 * sl
            AR[1, kv, h, :] = sl
            AR[2, kv, h, :] = -sl * kl
            ARC[0, kv, h, :] = 128 * sl
            ARC[1, kv, h, :] = 16 * sl
            ARC[2, kv, h, :] = sl * (15.5 - kl)
    c["ARtab"] = _bf(AR.reshape(3, 2, 512))
    c["ARCtab"] = _bf(ARC.reshape(3, 2, 512))
    k = np.arange(128)[:, None]
    q = np.arange(128)[None, :]
    tri = np.where(k <= q, 0.0, NEGM).astype(np.float32)
    low = np.where(k > q, 0.0, NEGM).astype(np.float32)
    full = np.full((128, 128), NEGM, np.float32)
    zero = np.zeros((128, 128), np.float32)
    sets = [low, zero, tri, full] if s == 0 else [full, low, zero, tri]
    nm = np.stack([np.tile(m, (1, 4)) for m in sets], axis=1)
    c["nsamask"] = _bf(nm)
    cm = np.full((NTO, 128, 2, 512), NEGM, np.float32)
    tk = np.zeros((NTO, 128, 128), np.float32)
    for i in range(NTO):
        qi = 2 * i + s
        nvis = (16 * i + 6) // 128 + 1
        t = 128 * qi + np.arange(128)
        for slot in range(2):
            cc = nvis - 2 + slot
            if cc < 0:
                continue
            n = 128 * cc + np.arange(128)
            vis = (16 * n[:, None] + 31) <= t[None, :]
            m = np.where(vis, 0.0, NEGM).astype(np.float32)
            cm[i, :, slot, :] = np.tile(m, (1, 4))
        cur = t // 64
        j = np.arange(128)[None, :]
        forced = (j == 0) | (j == cur[:, None]) | (j == cur[:, None] - 1)
        tk[i] = np.where(forced, 1e30, np.where(j <= cur[:, None], 0.0, -1e30))
    c["cmask"] = _bf(cm)
    c["tkmask"] = tk.astype(np.float32)
    ov = [1.0, 2.0, 2.0, 2.0, 1.0]
    M = np.zeros((512, 127), np.float32)
    for jj in range(127):
        for r in range(5):
            n = 4 * jj + r - 1
            if 0 <= n < 511:
                M[n, jj] = ov[r]
    c["Mmat"] = _bf(M.reshape(4, 128, 127))
    jj = np.arange(128)[:, None]
    ii = np.arange(128)[None, :]
    c["triI"] = np.where(jj <= ii, -1.0 / 16, 0.0).astype(np.float32)
    c["triA"] = np.where(jj > ii, -1.0 / 16, 0.0).astype(np.float32)
    c["cau"] = np.where(jj <= ii, 1.0, 0.0).astype(np.float32)
    m01 = np.zeros((128, 2), np.float32)
    m01[:, 0] = 1 - s
    m01[:, 1] = s
    c["m01"] = m01
    return c


COLS_A = np.concatenate([np.arange(1024, 1280), np.arange(1280, 1536), np.arange(1536, 1792), np.arange(2048, 2304),
                         np.arange(1792, 2048), np.arange(2304, 2560), np.arange(3096, 3608), np.arange(3608, 4632), np.arange(4632, 4648)])
COLS_B = np.concatenate([np.arange(0, 1024), np.arange(2584, 3096), np.arange(4648, 5672), np.arange(2560, 2584)])

_NC_CACHE = {}


def make_in_maps(inputs):
    f = lambda a: np.ascontiguousarray(np.asarray(a, np.float32))
    x = f(inputs["x"])
    mem = f(inputs["mem"])
    w_in = f(inputs["w_in"])[0]
    shared = {"w_kv": np.ascontiguousarray(w_in[:, COLS_A]), "w_q": np.ascontiguousarray(w_in[:, COLS_B])}
    for name in ("g_mix", "b_nsa_gate", "g_q", "g_kc", "g_ks", "g_kw", "g_nsa_out", "b_gk", "g_gla_out", "g_cross", "g_mem", "g_cq", "g_ck", "g_ffn"):
        shared[name] = f(inputs[name])
    shared["pe_kT"] = np.ascontiguousarray(f(inputs["pe_k"])[0].T)
    shared["pe_vT"] = np.ascontiguousarray(f(inputs["pe_v"])[0].T)
    for name in ("w_ck1", "w_ck2", "w_cv1", "w_cv2", "w_gk2", "w_out", "w_cq", "w_ck", "w_cv", "w_co", "w_gu", "w_down"):
        shared[name] = f(inputs[name])[0]
    consts = [const_tables(0), const_tables(1)]
    in_maps = []
    for c in range(8):
        b, s = c // 2, c % 2
        m = dict(shared)
        m.update(consts[s])
        m["x_all"] = x[b]
        m["x_own"] = np.ascontiguousarray(x[b].reshape(NTO, 2, 128, D)[:, s].reshape(TO, D))
        m["mem"] = mem[b]
        in_maps.append(m)
    return in_maps


def kernel(**inputs):
    if "nc" not in _NC_CACHE:
        _NC_CACHE["nc"] = build_program()
    nc = _NC_CACHE["nc"]
    in_maps = make_in_maps(inputs)
    res = run_bass_kernel_spmd(nc, in_maps, core_ids=list(range(8)))
    out = np.zeros((4, NTO, 2, 128, D), np.float32)
    for c in range(8):
        b, s = c // 2, c % 2
        out[b, :, s] = np.asarray(res.results[c]["out"], np.float32).reshape(NTO, 128, D)
    return out.reshape(4, T, D)
```
